# Optimizing a Trainium2 kernel written in Bass

```python
import math
import jax, jax.numpy as jnp
from jax import lax
import numpy as np

D_MODEL = 1024
BATCH = 2
SEQ = 8192
DEPTH = 2

DA_HEADS = 4
DA_QK_DIM = 64
DA_V_DIM = 2 * DA_QK_DIM
DA_ROT_DIM = DA_QK_DIM // 4
ROPE_THETA = 500000.0
Q_BLOCK = 128
RET_HEADS = 4
RET_QK_DIM = 32
RET_V_DIM = 2 * RET_QK_DIM
RET_THETA = 10000.0
RET_CHUNK = 128
POOL_GROUPS = 4
POOL_DIM = 64
POOL_WINDOWS = (2, 4, 8, 16)
DA_WIDTH = DA_HEADS * DA_V_DIM
RET_WIDTH = RET_HEADS * RET_V_DIM
POOL_WIDTH = POOL_GROUPS * POOL_DIM
MIX_WIDTH = DA_WIDTH + RET_WIDTH + POOL_WIDTH
SPLIT_SIZES = (DA_HEADS * 2 * DA_QK_DIM, DA_HEADS * 2 * DA_QK_DIM, DA_WIDTH,
               RET_HEADS * RET_QK_DIM, RET_HEADS * RET_QK_DIM, RET_WIDTH, RET_WIDTH,
               POOL_WIDTH)
IN_WIDTH = 2560
D_FF = 2816
CONV_WIDTH = 3
EPS = 1e-6

kernel_name = "hymba_style_diffattn_retention_pool_hybrid"


def rms_norm(x, g):
    xf = x.astype(jnp.float32)
    y = xf * lax.rsqrt(jnp.mean(xf * xf, axis=-1, keepdims=True) + EPS)
    return (y * g).astype(x.dtype)


def rotary(x, pos, rot_dim, theta):
    inv = jnp.float32(theta) ** (-jnp.arange(0, rot_dim, 2, dtype=jnp.float32) / rot_dim)
    ang = pos[:, None] * inv[None, :]
    cos = jnp.cos(ang).astype(x.dtype)
    sin = jnp.sin(ang).astype(x.dtype)
    half = rot_dim // 2
    x1, x2, xp = x[..., :half], x[..., half:rot_dim], x[..., rot_dim:]
    return jnp.concatenate([x1 * cos - x2 * sin, x2 * cos + x1 * sin, xp], axis=-1)


def diff_attention(q, k, v, lq1, lk1, lq2, lk2, subln_g, lam_init):
    B, S, _ = q.shape
    pos = jnp.arange(S, dtype=jnp.float32)
    q = q.reshape(B, S, DA_HEADS, 2, DA_QK_DIM).transpose(0, 2, 3, 1, 4)
    k = k.reshape(B, S, DA_HEADS, 2, DA_QK_DIM).transpose(0, 2, 3, 1, 4)
    q = rotary(q, pos, DA_ROT_DIM, ROPE_THETA)
    k = rotary(k, pos, DA_ROT_DIM, ROPE_THETA)
    v = v.reshape(B, S, DA_HEADS, DA_V_DIM).transpose(0, 2, 1, 3)
    lam = (jnp.exp(jnp.sum(lq1.astype(jnp.float32) * lk1.astype(jnp.float32)))
           - jnp.exp(jnp.sum(lq2.astype(jnp.float32) * lk2.astype(jnp.float32))) + lam_init)
    scale = DA_QK_DIM ** -0.5
    nb = S // Q_BLOCK
    qb = q.reshape(B, DA_HEADS, 2, nb, Q_BLOCK, DA_QK_DIM).transpose(3, 0, 1, 2, 4, 5)
    kpos = jnp.arange(S)

    def block(args):
        q_blk, i = args
        s = jnp.einsum('bhmqd,bhmkd->bhmqk', q_blk, k).astype(jnp.float32) * scale
        qpos = i * Q_BLOCK + jnp.arange(Q_BLOCK)
        mask = kpos[None, :] <= qpos[:, None]
        p = jax.nn.softmax(jnp.where(mask, s, -jnp.inf), axis=-1)
        a = p[:, :, 0] - lam * p[:, :, 1]
        return jnp.einsum('bhqk,bhkd->bhqd', a.astype(v.dtype), v)

    o = lax.map(block, (qb, jnp.arange(nb)))
    o = o.transpose(1, 0, 3, 2, 4).reshape(B, S, DA_HEADS, DA_V_DIM)
    o = rms_norm(o, subln_g) * (1.0 - lam_init)
    return o.reshape(B, S, DA_WIDTH)


def retention(q, k, v, g, ret_g):
    B, S, _ = q.shape
    H, dk, dv, C = RET_HEADS, RET_QK_DIM, RET_V_DIM, RET_CHUNK
    pos = jnp.arange(S, dtype=jnp.float32)
    q = rotary(q.reshape(B, S, H, dk).transpose(0, 2, 1, 3), pos, dk, RET_THETA)
    k = rotary(k.reshape(B, S, H, dk).transpose(0, 2, 1, 3), pos, dk, RET_THETA) * (dk ** -0.5)
    v = v.reshape(B, S, H, dv).transpose(0, 2, 1, 3)
    log_g = jnp.log(1.0 - 2.0 ** (-5.0 - jnp.arange(H, dtype=jnp.float32)))
    idx = jnp.arange(C, dtype=jnp.float32)
    diff = idx[:, None] - idx[None, :]
    decay = jnp.where(diff >= 0, jnp.exp(jnp.maximum(diff, 0.0) * log_g[:, None, None]), 0.0)
    xi = jnp.exp((idx + 1.0) * log_g[:, None])[..., None]
    zeta = jnp.exp((C - 1.0 - idx) * log_g[:, None])[..., None]
    chunk_decay = jnp.exp(C * log_g)[:, None, None]
    nc = S // C

    def to_chunks(t):
        return t.reshape(B, H, nc, C, t.shape[-1]).transpose(2, 0, 1, 3, 4).astype(jnp.float32)

    def step(state, inp):
        qi, ki, vi = inp
        inner = jnp.einsum('bhqd,bhkd->bhqk', qi, ki) * decay
        o = (jnp.einsum('bhqk,bhkv->bhqv', inner, vi)
             + jnp.einsum('bhqd,bhdv->bhqv', qi * xi, state))
        state = state * chunk_decay + jnp.einsum('bhkd,bhkv->bhdv', ki * zeta, vi)
        return state, o

    state0 = jnp.zeros((B, H, dk, dv), jnp.float32)
    _, o = lax.scan(step, state0, (to_chunks(q), to_chunks(k), to_chunks(v)))
    o = o.transpose(1, 0, 3, 2, 4).reshape(B, S, H, dv)
    mu = jnp.mean(o, axis=-1, keepdims=True)
    var = jnp.mean(jnp.square(o - mu), axis=-1, keepdims=True)
    o = (o - mu) * lax.rsqrt(var + EPS) * ret_g.reshape(H, dv)
    return jax.nn.silu(g) * o.reshape(B, S, RET_WIDTH).astype(g.dtype)


def pool_mixer(u, pool_w, pool_scale):
    B, S, _ = u.shape
    ug = u.reshape(B, S, POOL_GROUPS, POOL_DIM).astype(jnp.float32)
    c = jnp.cumsum(ug, axis=1)
    t = jnp.arange(S, dtype=jnp.float32)
    means = []
    for gi, w in enumerate(POOL_WINDOWS):
        cg = c[:, :, gi]
        shifted = jnp.pad(cg, ((0, 0), (w, 0), (0, 0)))[:, :S]
        means.append((cg - shifted) / jnp.minimum(t + 1.0, float(w))[:, None])
    pooled = (jnp.stack(means, axis=2) - ug).astype(u.dtype)
    y = jnp.einsum('bsgp,gpq->bsgq', pooled, pool_w)
    return y.reshape(B, S, POOL_WIDTH) * pool_scale


def conv_mlp(h, w_up, conv_w, conv_b, w_down):
    S = h.shape[1]
    u = h @ w_up
    up = jnp.pad(u, ((0, 0), (CONV_WIDTH - 1, 0), (0, 0)))
    uc = conv_b + sum(conv_w[j] * up[:, j:j + S] for j in range(CONV_WIDTH))
    gate, val = jnp.split(uc, 2, axis=-1)
    return (jax.nn.gelu(gate, approximate=True) * val) @ w_down


def setup_inputs(seed: int = 0) -> dict:
    key = jax.random.key(seed)
    ks = jax.random.split(key, 24)
    f32 = jnp.float32

    def nrm(k, shape, scale):
        return jax.random.normal(k, shape, f32) * scale

    def gain(k, shape):
        return 1.0 + 0.05 * jax.random.normal(k, shape, f32)

    L = DEPTH
    return {
        "x": nrm(ks[0], (BATCH, SEQ, D_MODEL), 1.0),
        "norm_mix_pre": gain(ks[1], (L, D_MODEL)),
        "norm_mix_post": gain(ks[2], (L, D_MODEL)),
        "w_in": nrm(ks[3], (L, D_MODEL, IN_WIDTH), D_MODEL ** -0.5),
        "lambda_q1": nrm(ks[4], (L, DA_QK_DIM), 0.1),
        "lambda_k1": nrm(ks[5], (L, DA_QK_DIM), 0.1),
        "lambda_q2": nrm(ks[6], (L, DA_QK_DIM), 0.1),
        "lambda_k2": nrm(ks[7], (L, DA_QK_DIM), 0.1),
        "diff_subln": gain(ks[8], (L, DA_V_DIM)),
        "ret_norm": gain(ks[9], (L, RET_WIDTH)),
        "pool_w": nrm(ks[10], (L, POOL_GROUPS, POOL_DIM, POOL_DIM), POOL_DIM ** -0.5),
        "pool_scale": 0.5 + 0.1 * jax.random.normal(ks[11], (L, POOL_WIDTH), f32),
        "w_out": nrm(ks[12], (L, MIX_WIDTH, D_MODEL), MIX_WIDTH ** -0.5),
        "norm_mlp_pre": gain(ks[13], (L, D_MODEL)),
        "norm_mlp_post": gain(ks[14], (L, D_MODEL)),
        "w_up": nrm(ks[15], (L, D_MODEL, 2 * D_FF), D_MODEL ** -0.5),
        "conv_w": nrm(ks[16], (L, CONV_WIDTH, 2 * D_FF), CONV_WIDTH ** -0.5),
        "conv_b": nrm(ks[17], (L, 2 * D_FF), 0.02),
        "w_down": nrm(ks[18], (L, D_FF, D_MODEL), D_FF ** -0.5),
    }


def reference(x, norm_mix_pre, norm_mix_post, w_in, lambda_q1, lambda_k1, lambda_q2, lambda_k2,
              diff_subln, ret_norm, pool_w, pool_scale, w_out, norm_mlp_pre, norm_mlp_post,
              w_up, conv_w, conv_b, w_down):
    split_points = []
    acc = 0
    for s in SPLIT_SIZES[:-1]:
        acc += s
        split_points.append(acc)
    for l in range(DEPTH):
        lam_init = 0.8 - 0.6 * math.exp(-0.3 * l)
        h = rms_norm(x, norm_mix_pre[l])
        proj = h @ w_in[l]
        q_da, k_da, v_da, q_r, k_r, v_r, g_r, u_pool = jnp.split(proj, split_points, axis=-1)
        o_da = diff_attention(q_da, k_da, v_da, lambda_q1[l], lambda_k1[l], lambda_q2[l],
                              lambda_k2[l], diff_subln[l], lam_init)
        o_ret = retention(q_r, k_r, v_r, g_r, ret_norm[l])
        o_pool = pool_mixer(u_pool, pool_w[l], pool_scale[l])
        mix = jnp.concatenate([o_da, o_ret, o_pool], axis=-1) @ w_out[l]
        x = x + rms_norm(mix, norm_mix_post[l])
        h = rms_norm(x, norm_mlp_pre[l])
        y = conv_mlp(h, w_up[l], conv_w[l], conv_b[l], w_down[l])
        x = x + rms_norm(y, norm_mlp_post[l])
    return x
```

```python
import math
from contextlib import ExitStack
import numpy as np
import concourse.bass as bass
import concourse.mybir as mybir
from concourse.bass_utils import run_bass_kernel_spmd

F32 = mybir.dt.float32
BF16 = mybir.dt.bfloat16
AF = mybir.ActivationFunctionType
ALU = mybir.AluOpType

D = 1024
S = 8192
NB = 2
DEPTH = 2
NR = 4
TOK = S // NR
HALO = 4
TCOL = TOK + HALO
DFF = 2816
NPAIR = DFF // 128
EPS = 1e-6
CW = 342
NCH_C = TCOL // CW
BLK_C = 2
QC = 256
OPAD = 256


class Res:
    __slots__ = ("name", "w", "r", "excl")

    def __init__(self, name="", excl=False):
        self.name = name
        self.w = None
        self.r = {}
        self.excl = excl


class _Eng:
    def __init__(self, name):
        self.name = name
        self.ops = []
        self.count = 0
        self.seen = {}


class Tracker:
    NDSEM = 8

    def __init__(self):
        self.eng = {n: _Eng(n) for n in ("pe", "act", "dve", "pool", "sp")}
        self.dma_n = {}
        self.dma_last = {}

    def _need(self, eng, tok, waits):
        if tok is None:
            return
        kind, key, val = tok
        k = (kind, key)
        if eng.seen.get(k, 0) >= val:
            return
        eng.seen[k] = val
        waits[k] = max(waits.get(k, 0), val)

    def _deps(self, eng, reads, writes):
        waits = {}
        for b in reads:
            self._need(eng, b.w, waits)
            if b.excl:
                for t in b.r.values():
                    if not (t[0] == 'c' and t[1] == eng.name):
                        self._need(eng, t, waits)
        for b in writes:
            if b.w is not None and not (b.w[0] == 'c' and b.w[1] == eng.name):
                self._need(eng, b.w, waits)
            for t in b.r.values():
                if t[0] == 'c' and t[1] == eng.name:
                    continue
                self._need(eng, t, waits)
        return waits

    def _commit(self, tok, reads, writes):
        for b in reads:
            b.r[(tok[0], tok[1])] = tok
        for b in writes:
            b.w = tok
            b.r = {}

    def op(self, engname, fn, reads=(), writes=(), inc=True):
        eng = self.eng[engname]
        waits = self._deps(eng, reads, writes)
        if engname == "pe":
            waits.pop(('c', 'pe'), None)
        for k, v in waits.items():
            eng.ops.append(('w', k, v))
        if inc:
            eng.count += 1
            tok = ('c', engname, eng.count)
            eng.ops.append(('o', fn, ('c', engname), 1))
        else:
            tok = ('c', engname, eng.count + 1)
            eng.ops.append(('o', fn, None, 0))
        self._commit(tok, reads, writes)
        return tok

    def dma(self, qname, fn, reads=(), writes=()):
        eng = self.eng[qname]
        n = self.dma_n.get(qname, 0)
        self.dma_n[qname] = n + 1
        slot = n % self.NDSEM
        rnd = n // self.NDSEM
        semkey = (qname, slot)
        waits = self._deps(eng, reads, writes)
        if rnd > 0:
            self._need(eng, ('d', semkey, 16 * rnd), waits)
        for k, v in waits.items():
            eng.ops.append(('w', k, v))
        tok = ('d', semkey, 16 * (rnd + 1))
        eng.ops.append(('o', fn, ('d', semkey), 16))
        self.dma_last[semkey] = tok
        self._commit(tok, reads, writes)
        return tok

    def wait_all(self, engname, toks):
        eng = self.eng[engname]
        waits = {}
        for t in toks:
            self._need(eng, t, waits)
        for k, v in waits.items():
            eng.ops.append(('w', k, v))

    def barrier(self):
        toks = [('c', n, e.count) for n, e in self.eng.items() if e.count > 0]
        toks += list(self.dma_last.values())
        for n in self.eng:
            self.wait_all(n, toks)

    def simulate(self):
        sem = {}
        pc = {n: 0 for n in self.eng}
        progress = True
        while progress:
            progress = False
            for n, e in self.eng.items():
                while pc[n] < len(e.ops):
                    o = e.ops[pc[n]]
                    if o[0] == 'w':
                        if sem.get(o[1], 0) >= o[2]:
                            pc[n] += 1
                            progress = True
                        else:
                            break
                    else:
                        if o[2] is not None:
                            sem[o[2]] = sem.get(o[2], 0) + o[3]
                        pc[n] += 1
                        progress = True
        stuck = {n: (pc[n], len(e.ops), e.ops[pc[n]][:3] if pc[n] < len(e.ops) else None) for n, e in self.eng.items() if pc[n] < len(e.ops)}
        return stuck, sem

    def build(self, nc, stack):
        sems = {}

        def sem(k):
            if k not in sems:
                nm = "s_" + "_".join(str(x) for x in (k[1] if isinstance(k[1], tuple) else (k[1],)))
                sems[k] = stack.enter_context(nc.semaphore(nm))
            return sems[k]
        for e in self.eng.values():
            for o in e.ops:
                if o[0] == 'w':
                    sem(o[1])
                elif o[2] is not None:
                    sem(o[2])
        block = stack.enter_context(nc.Block())

        def replay(h, ops):
            for o in ops:
                if o[0] == 'w':
                    h.wait_ge(sem(o[1]), o[2])
                else:
                    fn = o[1]
                    ins = getattr(h, fn[0])(**fn[1]) if isinstance(fn, tuple) else fn(h)
                    if o[2] is not None:
                        ins.then_inc(sem(o[2]), o[3])
        E = self.eng

        @block.tensor
        def _(h):
            replay(h, E["pe"].ops)

        @block.scalar
        def _(h):
            replay(h, E["act"].ops)

        @block.vector
        def _(h):
            replay(h, E["dve"].ops)

        @block.gpsimd
        def _(h):
            replay(h, E["pool"].ops)

        @block.sync
        def _(h):
            replay(h, E["sp"].ops)


ARENA_WORDS = 52000


class Ctx:
    def __init__(self, nc, stack):
        self.nc = nc
        self.T = Tracker()
        self.arena = stack.enter_context(nc.sbuf_tensor("arena", [128, ARENA_WORDS], F32))
        self.off = 0
        self.banks = [stack.enter_context(nc.psum_tensor(f"bank{i}", [128, 512], F32)) for i in range(8)]
        self.bank_res = [Res(f"bank{i}", excl=True) for i in range(8)]
        self.dram = {}

    def alloc(self, n, dt=F32, p0=0, p1=128):
        if dt == BF16:
            w = (n + 1) // 2
            a = self.arena[p0:p1, self.off:self.off + w].bitcast(BF16)
            if n % 2:
                a = a[:, 0:n]
        else:
            w = n
            a = self.arena[p0:p1, self.off:self.off + w]
        self.off += w
        assert self.off <= ARENA_WORDS, f"arena overflow {self.off}"
        return a

    def alloc_at(self, off_words, n, dt=F32, p0=0, p1=128):
        if dt == BF16:
            w = (n + 1) // 2
            a = self.arena[p0:p1, off_words:off_words + w].bitcast(BF16)
        else:
            w = n
            a = self.arena[p0:p1, off_words:off_words + w]
        assert off_words + w <= ARENA_WORDS
        return a

    def mark(self):
        return self.off

    def release(self, m):
        self.T.barrier()
        self.off = m

    def din(self, name, shape, dt=F32):
        t = self.nc.dram_tensor(name, list(shape), dt, kind="ExternalInput").ap()
        self.dram[name] = t
        return t

    def dout(self, name, shape, dt=F32):
        t = self.nc.dram_tensor(name, list(shape), dt, kind="ExternalOutput").ap()
        self.dram[name] = t
        return t

    def dscratch(self, name, shape, dt=F32):
        t = self.nc.dram_tensor(name, list(shape), dt, kind="Internal").ap()
        self.dram[name] = t
        return t


def _mm(T, out, lhsT, rhs, start, stop, reads, writes, inc=True):
    T.op("pe", ("matmul", dict(out=out, lhsT=lhsT, rhs=rhs, start=start, stop=stop)), reads=reads, writes=writes, inc=inc)


class Consts:
    def __init__(self, cx):
        T = cx.T
        self.res = Res("consts")
        self.ones_bf = cx.alloc(128, BF16)
        self.onesF = cx.alloc(128)
        self.ones64 = cx.alloc(64)
        self.ones32 = cx.alloc(128)
        self.col = cx.alloc(8)
        T.op("pool", ("memset", dict(ap=self.ones_bf, constant=1.0)), writes=[self.res])
        T.op("pool", ("memset", dict(ap=self.onesF, constant=1.0 / 128.0)), writes=[self.res])
        T.op("pool", ("memset", dict(ap=self.ones64, constant=1.0 / 64.0)), writes=[self.res])
        T.op("pool", ("memset", dict(ap=self.ones32, constant=1.0)), writes=[self.res])
        vals = [EPS * D, EPS, 1.0, 32.0]
        for i, v in enumerate(vals):
            T.op("pool", ("memset", dict(ap=self.col[:, i:i + 1], constant=v)), writes=[self.res])
        self.eps_d = self.col[:, 0:1]
        self.eps = self.col[:, 1:2]
        self.one = self.col[:, 2:3]


def load_gains(cx, K, dram_g, n):
    T = cx.T
    g = cx.alloc(n)
    r = Res("gains")
    T.dma("sp", ("dma_start", dict(out=g, in_=dram_g)), writes=[r])
    T.op("dve", ("tensor_scalar", dict(out=g, in0=g, scalar1=32.0, scalar2=None, op0=ALU.mult)), reads=[r], writes=[r])
    return g, r


def xres_for(xres, c0, n):
    bw = CW * BLK_C
    return [xres[b] for b in range(len(xres)) if b * bw < c0 + n and (b + 1) * bw > c0]


def emit_phase_A(cx, K, xT, xres, g_ap, g_res, hT_out, hT_res):
    T = cx.T
    m = cx.mark()
    NCH = 4
    W = 512
    sq = [cx.alloc(W, BF16) for _ in range(2)]
    sq_res = [Res("A_sq0"), Res("A_sq1")]
    rstd = [cx.alloc(W) for _ in range(2)]
    rstd_res = [Res("A_rstd0"), Res("A_rstd1")]
    hbuf = [cx.alloc(8 * W, BF16).rearrange("p (c n) -> p c n", c=8) for _ in range(2)]
    hbuf_res = [[Res(f"A_h{i}_{c}") for c in range(8)] for i in range(2)]
    hview = hT_out.rearrange("(c p) n -> p c n", p=128)
    for ch in range(NCH):
        c0 = HALO + ch * W
        xr = xres_for(xres, c0, W)
        bank = cx.banks[ch % 2][:, 0:W]
        bres = cx.bank_res[ch % 2]
        for c in range(8):
            s = sq[c % 2]
            sr = sq_res[c % 2]
            T.op("act", ("activation", dict(out=s, in_=xT[:, c, c0:c0 + W], func=AF.Square)),
                 reads=xr, writes=[sr])
            _mm(T, bank, K.ones_bf, s, c == 0, c == 7, [sr, K.res], [bres])
        r = rstd[ch % 2]
        rr = rstd_res[ch % 2]
        T.op("act", ("activation", dict(out=r, in_=bank, func=AF.Sqrt, bias=K.eps_d, scale=1.0)),
             reads=[bres, K.res], writes=[rr])
        T.op("dve", ("reciprocal", dict(out=r, in_=r)), reads=[rr], writes=[rr])
        hb = hbuf[ch % 2]
        hr = hbuf_res[ch % 2]
        for c in range(8):
            T.op("dve", ("scalar_tensor_tensor", dict(
                out=hb[:, c, :], in0=xT[:, c, c0:c0 + W], scalar=g_ap[:, c:c + 1], in1=r,
                op0=ALU.mult, op1=ALU.mult)), reads=xr + [rr, g_res], writes=[hr[c]])
        T.dma("sp", ("dma_start", dict(out=hview[:, :, ch * W:(ch + 1) * W], in_=hb)),
              reads=hr, writes=[hT_res])
    cx.release(m)


def emit_phase_C(cx, K, xT, xres, oc_src, oc_res, wout, wup, wdown, cw_d, gains_d, mask_d):
    T = cx.T
    m = cx.mark()
    BW = CW * BLK_C
    NBLK = NCH_C // BLK_C
    wres = Res("C_wdram")
    cw = cx.alloc(NPAIR * 8).rearrange("p (j g k) -> p j g k", j=NPAIR, g=2)
    cw_res = Res("C_cw")
    T.dma("sp", ("dma_start", dict(out=cw, in_=cw_d.rearrange("p (j g k) -> p j g k", j=NPAIR, g=2))), writes=[cw_res])
    gains, g_res = load_gains(cx, K, gains_d, 24)
    msk = cx.alloc(8)
    msk_res = Res("C_mask")
    T.dma("sp", ("dma_start", dict(out=msk, in_=mask_d)), writes=[msk_res])
    ocb = cx.alloc(8 * BW, BF16).rearrange("p (c n) -> p c n", c=8)
    ocb_res = Res("C_ocb")
    ysb = cx.alloc(8 * BW).rearrange("p (c n) -> p c n", c=8)
    ysb_res = [[Res(f"C_ysb{d}_{k}") for k in range(BLK_C)] for d in range(8)]
    sqb = [cx.alloc(CW, BF16) for _ in range(2)]
    sqb_res = [Res("C_sq0"), Res("C_sq1")]
    rstd = [cx.alloc(CW) for _ in range(2)]
    rstd_res = [Res("C_rstd0"), Res("C_rstd1")]
    hb = cx.alloc(8 * (BW + 2), BF16).rearrange("p (c n) -> p c n", c=8)
    hb_res = [[Res(f"C_hb{c}_{k}") for k in range(BLK_C)] for c in range(8)]
    hprev = cx.alloc(8 * 2, BF16).rearrange("p (c n) -> p c n", c=8)
    hprev_res = Res("C_hprev")
    hb_pre_res = Res("C_hbpre")
    act = cx.alloc(NPAIR * BW, BF16).rearrange("p (j n) -> p j n", j=NPAIR)
    wup_s = [cx.alloc(8 * 256, BF16).rearrange("p (c n) -> p c n", c=8) for _ in range(3)]
    wup_res = [Res(f"C_wup{i}") for i in range(3)]
    wdn_s = [cx.alloc(NPAIR * 128, BF16).rearrange("p (k n) -> p k n", k=NPAIR) for _ in range(2)]
    wdn_res = [Res(f"C_wdn{i}") for i in range(2)]
    wo_s = [cx.alloc(8 * 128, BF16).rearrange("p (k n) -> p k n", k=8) for _ in range(2)]
    wo_res = [Res(f"C_wo{i}") for i in range(2)]
    tbuf = [cx.alloc(CW) for _ in range(4)]
    tbuf_res = [Res(f"C_t{i}") for i in range(4)]
    glb = [cx.alloc(CW) for _ in range(2)]
    glb_res = [Res(f"C_gl{i}") for i in range(2)]
    T.op("pool", ("memset", dict(ap=hprev, constant=0.0)), writes=[hprev_res])

    nwo = [0]
    nwd = [0]
    nwu = [0]
    nt = [0]
    ngl = [0]
    nsq = [0]

    def norm_and_residual(blk, k, gidx):
        cc = slice(k * CW, (k + 1) * CW)
        xc = slice(blk * BW + k * CW, blk * BW + (k + 1) * CW)
        bank = cx.banks[2][:, 0:CW]
        bres = cx.bank_res[2]
        for d in range(8):
            i = nsq[0] % 2
            nsq[0] += 1
            T.op("pool", ("tensor_tensor", dict(out=sqb[i], in0=ysb[:, d, cc], in1=ysb[:, d, cc], op=ALU.mult)),
                 reads=[ysb_res[d][k]], writes=[sqb_res[i]])
            _mm(T, bank, K.ones_bf, sqb[i], d == 0, d == 7, [sqb_res[i], K.res], [bres])
        r = rstd[k]
        rr = rstd_res[k]
        T.op("act", ("activation", dict(out=r, in_=bank, func=AF.Sqrt, bias=K.eps_d, scale=1.0)),
             reads=[bres, K.res], writes=[rr])
        T.op("dve", ("reciprocal", dict(out=r, in_=r)), reads=[rr], writes=[rr])
        for d in range(8):
            T.op("dve", ("scalar_tensor_tensor", dict(
                out=ysb[:, d, cc], in0=ysb[:, d, cc], scalar=gains[:, gidx * 8 + d:gidx * 8 + d + 1], in1=r,
                op0=ALU.mult, op1=ALU.mult)), reads=[ysb_res[d][k], rr, g_res], writes=[ysb_res[d][k]])
            T.op("dve", ("tensor_tensor", dict(out=xT[:, d, xc], in0=xT[:, d, xc], in1=ysb[:, d, cc], op=ALU.add)),
                 reads=[ysb_res[d][k], xres[blk]], writes=[xres[blk]])

    for blk in range(NBLK):
        b0 = blk * BW
        T.dma("sp", ("dma_start", dict(out=ocb, in_=oc_src[:, :, b0:b0 + BW])), reads=[oc_res], writes=[ocb_res])
        for d in range(8):
            si = nwo[0] % 2
            nwo[0] += 1
            T.dma("pool", ("dma_start", dict(out=wo_s[si], in_=wout[d])), reads=[wres], writes=[wo_res[si]])
            for k in range(BLK_C):
                bank = cx.banks[(d * BLK_C + k) % 2][:, 0:CW]
                bres = cx.bank_res[(d * BLK_C + k) % 2]
                for kt in range(8):
                    _mm(T, bank, wo_s[si][:, kt, :], ocb[:, kt, k * CW:(k + 1) * CW], kt == 0, kt == 7,
                        [wo_res[si], ocb_res], [bres], inc=(kt == 7))
                T.op("act", ("activation", dict(out=ysb[:, d, k * CW:(k + 1) * CW], in_=bank, func=AF.Copy)),
                     reads=[bres], writes=[ysb_res[d][k]])
        for k in range(BLK_C):
            norm_and_residual(blk, k, 0)
        T.op("pool", ("tensor_copy", dict(out=hb[:, :, 0:2], in_=hprev)), reads=[hprev_res],
             writes=[hb_pre_res])
        for k in range(BLK_C):
            xc = slice(b0 + k * CW, b0 + (k + 1) * CW)
            bank = cx.banks[2][:, 0:CW]
            bres = cx.bank_res[2]
            for d in range(8):
                i = nsq[0] % 2
                nsq[0] += 1
                T.op("pool", ("tensor_tensor", dict(out=sqb[i], in0=xT[:, d, xc], in1=xT[:, d, xc], op=ALU.mult)),
                     reads=[xres[blk]], writes=[sqb_res[i]])
                _mm(T, bank, K.ones_bf, sqb[i], d == 0, d == 7, [sqb_res[i], K.res], [bres])
            r = rstd[k]
            rr = rstd_res[k]
            T.op("act", ("activation", dict(out=r, in_=bank, func=AF.Sqrt, bias=K.eps_d, scale=1.0)),
                 reads=[bres, K.res], writes=[rr])
            T.op("dve", ("reciprocal", dict(out=r, in_=r)), reads=[rr], writes=[rr])
            for d in range(8):
                T.op("dve", ("scalar_tensor_tensor", dict(
                    out=hb[:, d, 2 + k * CW:2 + (k + 1) * CW], in0=xT[:, d, xc], scalar=gains[:, 8 + d:9 + d], in1=r,
                    op0=ALU.mult, op1=ALU.mult)), reads=[xres[blk], rr, g_res], writes=[hb_res[d][k]])
            if blk == 0 and k == 0:
                for d in range(8):
                    T.op("pool", ("tensor_scalar", dict(out=hb[:, d, 2:2 + HALO], in0=hb[:, d, 2:2 + HALO],
                                                                 scalar1=msk[:, 0:1], scalar2=None, op0=ALU.mult)),
                         reads=[hb_res[d][0], msk_res], writes=[hb_res[d][0]])
        allhb = [hb_res[d][k] for d in range(8) for k in range(BLK_C)]
        T.op("pool", ("tensor_copy", dict(out=hprev, in_=hb[:, :, BW:BW + 2])), reads=allhb, writes=[hprev_res])
        act_res = [[Res(f"C_act{j}_{k}") for k in range(BLK_C)] for j in range(NPAIR)]
        for j in range(NPAIR):
            si = nwu[0] % 3
            nwu[0] += 1
            T.dma("pool", ("dma_start", dict(out=wup_s[si], in_=wup[j])), reads=[wres], writes=[wup_res[si]])
            for k in range(BLK_C):
                hc = slice(k * CW, k * CW + CW + 2)
                rd = [hb_res[d][k] for d in range(8)] + ([hb_pre_res] if k == 0 else [hb_res[d][k - 1] for d in range(8)])
                pb = 3 + 2 * ((j * BLK_C + k) % 2)
                tt = []
                for gv in range(2):
                    bank = cx.banks[pb + gv][:, 0:CW + 2]
                    bres = cx.bank_res[pb + gv]
                    for c in range(8):
                        _mm(T, bank, wup_s[si][:, c, gv * 128:(gv + 1) * 128], hb[:, c, hc], c == 0, c == 7,
                            [wup_res[si]] + rd, [bres], inc=(c == 7))
                    ti = nt[0] % 4
                    nt[0] += 1
                    t = tbuf[ti]
                    tr = tbuf_res[ti]
                    T.op("act", ("activation", dict(
                        out=t, in_=bank[:, 2:CW + 2], func=AF.Identity, bias=cw[:, j, gv, 3:4], scale=cw[:, j, gv, 2:3])),
                        reads=[bres, cw_res], writes=[tr])
                    T.op("dve", ("scalar_tensor_tensor", dict(
                        out=t, in0=bank[:, 1:CW + 1], scalar=cw[:, j, gv, 1:2], in1=t, op0=ALU.mult, op1=ALU.add)),
                        reads=[bres, cw_res, tr], writes=[tr])
                    T.op("dve", ("scalar_tensor_tensor", dict(
                        out=t, in0=bank[:, 0:CW], scalar=cw[:, j, gv, 0:1], in1=t, op0=ALU.mult, op1=ALU.add)),
                        reads=[bres, cw_res, tr], writes=[tr])
                    tt.append((t, tr))
                gi = ngl[0] % 2
                ngl[0] += 1
                T.op("act", ("activation", dict(out=glb[gi], in_=tt[0][0], func=AF.Gelu_apprx_tanh)),
                     reads=[tt[0][1]], writes=[glb_res[gi]])
                T.op("pool", ("tensor_tensor", dict(
                    out=act[:, j, k * CW:(k + 1) * CW], in0=glb[gi], in1=tt[1][0], op=ALU.mult)),
                    reads=[glb_res[gi], tt[1][1]], writes=[act_res[j][k]])
        for d in range(8):
            si = nwd[0] % 2
            nwd[0] += 1
            T.dma("pool", ("dma_start", dict(out=wdn_s[si], in_=wdown[d])), reads=[wres], writes=[wdn_res[si]])
            for k in range(BLK_C):
                bank = cx.banks[(d * BLK_C + k) % 2][:, 0:CW]
                bres = cx.bank_res[(d * BLK_C + k) % 2]
                for j in range(NPAIR):
                    _mm(T, bank, wdn_s[si][:, j, :], act[:, j, k * CW:(k + 1) * CW], j == 0, j == NPAIR - 1,
                        [wdn_res[si], act_res[j][k]], [bres], inc=(j == NPAIR - 1))
                T.op("act", ("activation", dict(out=ysb[:, d, k * CW:(k + 1) * CW], in_=bank, func=AF.Copy)),
                     reads=[bres], writes=[ysb_res[d][k]])
        for k in range(BLK_C):
            norm_and_residual(blk, k, 2)
    cx.release(m)


CB_DECAY = 0
CB_ZETA = 128
CB_CD = 129
CB_LAMI = 130
CB_OML = 131
CB_SUBLN = 132
CB_PSCALE = 133
CB_RETG = 134
CB_XI = 135
CB_BCUR = CB_XI + 512
CB_BPREV = CB_BCUR + 128
CB_BCUR0 = CB_BPREV + 128
CB_LAMV = CB_BCUR0 + 128
CB_N = CB_LAMV + 256
CBB_I32 = 0
CBB_MASK = 32
CBB_PW = 32 + 1024
CBB_N = CBB_PW + 64


def emit_phase_B(cx, K, hT_full, h_res, wh_d, rot_tab, cb_d, cbb_d, ocat_out, ocat_res, parts="123e"):
    T = cx.T
    m = cx.mark()
    NCH = 16
    W = 512
    cres = Res("B_cdram")
    cb = cx.alloc(CB_N)
    cb_res = Res("B_cb")
    T.dma("sp", ("dma_start", dict(out=cb, in_=cb_d)), reads=[cres], writes=[cb_res])
    cbb = cx.alloc(CBB_N, BF16)
    cbb_res = Res("B_cbb")
    T.dma("pool", ("dma_start", dict(out=cbb, in_=cbb_d)), reads=[cres], writes=[cbb_res])
    decayT = cb[:, CB_DECAY:CB_DECAY + 128]
    zeta = cb[:, CB_ZETA:CB_ZETA + 1]
    cd = cb[0:32, CB_CD:CB_CD + 1]
    xi4 = cb[0:32, CB_XI:CB_XI + 512]
    bcur = cb[:, CB_BCUR:CB_BCUR + 128]
    bprev = cb[:, CB_BPREV:CB_BPREV + 128]
    bcur0 = cb[:, CB_BCUR0:CB_BCUR0 + 128]
    i32 = cbb[0:32, CBB_I32:CBB_I32 + 32]
    masks = [cbb[:, CBB_MASK + i * 512:CBB_MASK + (i + 1) * 512] for i in range(2)]
    pw = cbb[0:64, CBB_PW:CBB_PW + 64]

    qcat = cx.alloc(2 * S, BF16).rearrange("p (q m n) -> p q m n", m=2, n=QC)
    kT = cx.alloc(S, BF16)
    qz_res = Res("B_qzero")
    T.op("pool", ("memset", dict(ap=qcat[64:128, :, 0, :], constant=0.0)), writes=[qz_res])
    T.op("pool", ("memset", dict(ap=qcat[0:64, :, 1, :], constant=0.0)), writes=[qz_res])
    VV = cx.alloc(64 * 192, BF16).rearrange("p (k n) -> p k n", k=64)
    goff = cx.off
    cx.off += 8192
    assert cx.off <= ARENA_WORDS
    gT = cx.alloc_at(goff, 8192, F32, 64, 128)
    RQ = cx.alloc_at(goff, 8192, BF16, 0, 32)
    RK = cx.alloc_at(goff + 4096, 8192, BF16, 0, 32)
    QXI = cx.alloc(S, BF16, 0, 32)
    q_res = [Res(f"B_q{i}") for i in range(NCH)]
    q1_res = [Res(f"B_q1{i}") for i in range(NCH)]
    k_res = [Res(f"B_k{i}") for i in range(NCH)]
    vv_res = [Res(f"B_vv{i}") for i in range(NCH)]
    g_res = [Res(f"B_g{i}") for i in range(NCH)]
    rq_res = [Res(f"B_rq{i}") for i in range(NCH)]
    rk_res = [Res(f"B_rk{i}") for i in range(NCH)]
    qxi_res = [Res(f"B_qxi{i}") for i in range(NCH)]

    zp = cx.alloc(OPAD, BF16)
    zp_res = Res("B_zp")
    T.op("pool", ("memset", dict(ap=zp, constant=0.0)), writes=[zp_res])
    for h in range(2):
        T.dma("sp", ("dma_start", dict(out=ocat_out[h * 128:(h + 1) * 128, 0:OPAD], in_=zp)), reads=[zp_res], writes=[ocat_res])

    m1 = cx.mark()
    wh = cx.alloc(8 * 1024, BF16).rearrange("p (c n) -> p c n", c=8)
    wh_res = [Res(f"B_wh{c}") for c in range(8)]
    whv = wh_d.rearrange("(c p) n -> p c n", p=128)
    for c in range(8):
        T.dma("pool", ("dma_start", dict(out=wh[:, c, :], in_=whv[:, c, :])), reads=[cres], writes=[wh_res[c]])
    hcb = [cx.alloc(8 * W, BF16).rearrange("p (c n) -> p c n", c=8) for _ in range(2)]
    hcb_res = [Res("B_hc0"), Res("B_hc1")]
    tabb = [cx.alloc(4 * W).rearrange("p (s n) -> p s n", s=4) for _ in range(2)]
    tabb_res = [Res("B_tab0"), Res("B_tab1")]
    tAb = [cx.alloc(W) for _ in range(2)]
    tBb = [cx.alloc(W) for _ in range(2)]
    tA_res = [Res("B_tA0"), Res("B_tA1")]
    tB_res = [Res("B_tB0"), Res("B_tB1")]
    ucur = [cx.alloc(4 * 64).rearrange("p (t n) -> p t n", t=4) for _ in range(2)]
    ucur_res = [[Res(f"B_u{i}_{t}") for t in range(4)] for i in range(2)]
    pooled = cx.alloc(W, BF16)
    pooled_res = Res("B_pooled")
    pstage = [cx.alloc(W, BF16) for _ in range(2)]
    pstage_res = [Res("B_ps0"), Res("B_ps1")]
    nrot = [0]

    def rotary(ps, ps_sw, bres, bres_sw, tab, tab_r, cslot, sslot, p0, p1, outs):
        i = nrot[0] % 2
        nrot[0] += 1
        tA, tB = tAb[i], tBb[i]
        T.op("dve", ("tensor_tensor", dict(out=tA[p0:p1, :], in0=ps_sw[p0:p1, :], in1=tab[p0:p1, sslot, :], op=ALU.mult)),
             reads=[bres_sw, tab_r], writes=[tA_res[i]])
        T.op("dve", ("tensor_tensor", dict(out=tB[p0:p1, :], in0=ps[p0:p1, :], in1=tab[p0:p1, cslot, :], op=ALU.mult)),
             reads=[bres, tab_r], writes=[tB_res[i]])
        for dst, s0, s1, r in outs:
            a_, b_ = tA[s0:s1, :], tB[s0:s1, :]
            if len(dst.shape) == 3:
                a_ = a_.rearrange("p (a n) -> p a n", a=2)
                b_ = b_.rearrange("p (a n) -> p a n", a=2)
            T.op("pool", ("tensor_tensor", dict(out=dst, in0=a_, in1=b_, op=ALU.add)),
                 reads=[tA_res[i], tB_res[i]], writes=[r])

    for ch in range(NCH):
        rank, lc = ch // 4, (ch % 4) * W
        cols = slice(ch * W, (ch + 1) * W)
        hc, hcr = hcb[ch % 2], hcb_res[ch % 2]
        tab, tabr = tabb[ch % 2], tabb_res[ch % 2]
        hv = hT_full[rank].rearrange("(c p) n -> p c n", p=128)
        T.dma("sp", ("dma_start", dict(out=hc, in_=hv[:, :, lc:lc + W])), reads=[h_res], writes=[hcr])
        T.dma("sp", ("dma_start", dict(out=tab, in_=rot_tab[ch].rearrange("p (s n) -> p s n", s=4))),
              reads=[cres], writes=[tabr])
        for ti in range(6):
            bank = cx.banks[ti][:, :]
            for c in range(8):
                _mm(T, bank, wh[:, c, ti * 128:(ti + 1) * 128], hc[:, c, :], c == 0, c == 7,
                    [wh_res[c], hcr], [cx.bank_res[ti]], inc=(c == 7))
        B = cx.banks
        BR = cx.bank_res
        rotary(B[0], B[1], BR[0], BR[1], tab, tabr, 0, 1, 0, 128,
               [(qcat[0:64, 2 * ch:2 * ch + 2, 0, :], 0, 64, q_res[ch]), (qcat[64:128, 2 * ch:2 * ch + 2, 1, :], 64, 128, q1_res[ch])])
        rotary(B[2], B[3], BR[2], BR[3], tab, tabr, 0, 1, 0, 128, [(kT[:, cols], 0, 128, k_res[ch])])
        T.op("act", ("activation", dict(out=gT[:, cols], in_=B[4][64:128, :], func=AF.Copy)),
             reads=[BR[4]], writes=[g_res[ch]])
        rotary(B[4], B[5], BR[4], BR[5], tab, tabr, 2, 3, 0, 64,
               [(RQ[:, cols], 0, 32, rq_res[ch]), (RK[:, cols], 32, 64, rk_res[ch])])
        T.op("pool", ("tensor_tensor", dict(out=QXI[:, cols], in0=RQ[:, cols], in1=xi4, op=ALU.mult)),
             reads=[rq_res[ch], cb_res], writes=[qxi_res[ch]])
        uc, ucr = ucur[ch % 2], ucur_res[ch % 2]
        for tt in range(4):
            bank = B[6 + tt // 2][:, (tt % 2) * 256:(tt % 2) * 256 + 256]
            bres = BR[6 + tt // 2]
            for c in range(8):
                _mm(T, bank, hc[:, c, tt * 128:(tt + 1) * 128], wh[:, c, 768:1024], c == 0, c == 7,
                    [wh_res[c], hcr], [bres], inc=(c == 7))
            kt = ch * 4 + tt
            T.op("act", ("activation", dict(out=VV[:, kt, :], in_=bank[:, 0:192], func=AF.Copy)),
                 reads=[bres], writes=[vv_res[ch]])
            T.op("dve", ("tensor_copy", dict(out=uc[:, tt, :], in_=bank[:, 192:256])),
                 reads=[bres], writes=[ucr[tt]])
        for tt in range(4):
            tile = ch * 4 + tt
            out = B[0][0:64, tt * 128:(tt + 1) * 128]
            _mm(T, out, uc[:, tt, :], bcur0 if tile == 0 else bcur, True, tile == 0, [ucr[tt], cb_res], [BR[0]])
            if tile > 0:
                if tt > 0:
                    prev, prevr = uc[:, tt - 1, :], ucr[tt - 1]
                else:
                    prev, prevr = ucur[(ch - 1) % 2][:, 3, :], ucur_res[(ch - 1) % 2][3]
                _mm(T, out, prev, bprev, False, True, [prevr, cb_res], [BR[0]])
        T.op("act", ("activation", dict(out=pooled[0:64, :], in_=B[0][0:64, :], func=AF.Copy)), reads=[BR[0]], writes=[pooled_res])
        _mm(T, B[1][0:64, :], pw, pooled[0:64, :], True, True, [cbb_res, pooled_res], [BR[1]])
        pst, pstr = pstage[ch % 2], pstage_res[ch % 2]
        T.op("act", ("activation", dict(out=pst[0:64, :], in_=B[1][0:64, :], func=AF.Copy,
                                                      scale=cb[0:64, CB_PSCALE:CB_PSCALE + 1])),
             reads=[BR[1], cb_res], writes=[pstr])
        T.dma("sp", ("dma_start", dict(out=ocat_out[192:256, OPAD + ch * W:OPAD + (ch + 1) * W], in_=pst[0:64, :])),
              reads=[pstr], writes=[ocat_res])
    cx.release(m1)

    lw = cx.alloc(128)
    sc = cx.alloc(8)
    sc_res = Res("B_sc")
    lv = cb[:, CB_LAMV:CB_LAMV + 256]
    T.op("dve", ("tensor_tensor", dict(out=lw[:, 0:64], in0=lv[:, 0:64], in1=lv[:, 64:128], op=ALU.mult)), reads=[cb_res], writes=[sc_res])
    T.op("dve", ("tensor_tensor", dict(out=lw[:, 64:128], in0=lv[:, 128:192], in1=lv[:, 192:256], op=ALU.mult)), reads=[cb_res], writes=[sc_res])
    T.op("dve", ("reduce_sum", dict(out=sc[:, 0:1], in_=lw[:, 0:64], axis=mybir.AxisListType.X)), reads=[sc_res], writes=[sc_res])
    T.op("dve", ("reduce_sum", dict(out=sc[:, 1:2], in_=lw[:, 64:128], axis=mybir.AxisListType.X)), reads=[sc_res], writes=[sc_res])
    T.op("act", ("activation", dict(out=sc[:, 2:4], in_=sc[:, 0:2], func=AF.Exp)), reads=[sc_res], writes=[sc_res])
    T.op("dve", ("tensor_tensor", dict(out=sc[:, 4:5], in0=sc[:, 3:4], in1=sc[:, 2:3], op=ALU.subtract)), reads=[sc_res], writes=[sc_res])
    T.op("dve", ("tensor_tensor", dict(out=sc[:, 4:5], in0=sc[:, 4:5], in1=cb[:, CB_LAMI:CB_LAMI + 1], op=ALU.subtract)),
         reads=[sc_res, cb_res], writes=[sc_res])
    T.op("dve", ("tensor_tensor", dict(out=sc[:, 5:6], in0=cb[:, CB_SUBLN:CB_SUBLN + 1], in1=cb[:, CB_OML:CB_OML + 1], op=ALU.mult)),
         reads=[cb_res], writes=[sc_res])
    neg_lam = sc[:, 4:5]
    gda = sc[:, 5:6]

    m2 = cx.mark()
    NP = 6
    Pb = [cx.alloc(W, BF16) for _ in range(NP)]
    P_res = [Res(f"B_P{i}") for i in range(NP)]
    Rb = cx.alloc(W)
    tb = cx.alloc(W)
    ob = cx.alloc(QC)
    sqo = cx.alloc(QC)
    l1 = cx.alloc(QC)
    ostage = [cx.alloc(W, BF16) for _ in range(2)]
    ost_res = [[Res(f"B_ost{i}_{h}") for h in range(2)] for i in range(2)]
    ep_res = Res("B_ep")
    Lacc = [cx.alloc(W), cx.alloc(W)]
    lacc_res = [Res("B_lacc0"), Res("B_lacc1")]
    innerM = [cx.alloc(128, BF16) for _ in range(2)]
    inner_res = [Res("B_in0"), Res("B_in1")]
    kz = [cx.alloc(32, BF16) for _ in range(2)]
    kz_res = [Res("B_kz0"), Res("B_kz1")]
    state_f = cx.alloc(64)
    state_bf = cx.alloc(64, BF16)
    ogb2 = cx.alloc(128)
    ogb2_res = Res("B_ogb2")
    st_res = Res("B_stf")
    stb_res = Res("B_stb")
    og = [cx.alloc(W) for _ in range(2)]
    og_res = [[Res(f"B_og{i}_{j}") for j in range(4)] for i in range(2)]
    oc = cx.alloc(W)
    sq = cx.alloc(W)
    rs = cx.alloc(W)
    g0 = cx.alloc(W)
    e1 = cx.alloc(W)
    t1 = cx.alloc(W)
    rstage = [cx.alloc(W, BF16) for _ in range(2)]
    rst_res = [Res("B_rst0"), Res("B_rst1")]
    gn_res = Res("B_gn")
    B = cx.banks
    BR = cx.bank_res
    T.op("pool", ("memset", dict(ap=state_f[0:32, :], constant=0.0)), writes=[st_res])
    T.op("pool", ("memset", dict(ap=state_bf[0:32, :], constant=0.0)), writes=[stb_res])

    def ret_gen():
        for i in range(64):
            ch = i // 4
            cs = slice(i * 128, (i + 1) * 128)
            im, imr = innerM[i % 2], inner_res[i % 2]
            kzb, kzr = kz[i % 2], kz_res[i % 2]
            _mm(T, B[4][:, 0:128], RK[:, cs], RQ[:, cs], True, True, [rk_res[ch], rq_res[ch]], [BR[4]], inc=False)
            _mm(T, B[4][:, 128:160], RK[:, cs], i32, True, True, [rk_res[ch], cbb_res], [BR[4]], inc=True)
            yield
            T.op("dve", ("tensor_tensor", dict(out=im, in0=B[4][:, 0:128], in1=decayT, op=ALU.mult)),
                 reads=[BR[4], cb_res], writes=[imr])
            T.op("dve", ("tensor_scalar", dict(out=kzb, in0=B[4][:, 128:160], scalar1=zeta, scalar2=None, op0=ALU.mult)),
                 reads=[BR[4], cb_res], writes=[kzr])
            yield
            _mm(T, B[5][0:64, 0:128], VV[:, i, 128:192], im, True, True, [vv_res[ch], imr], [BR[5]], inc=False)
            _mm(T, B[5][0:32, 128:192], kzb, VV[:, i, 128:192], True, True, [kzr, vv_res[ch]], [BR[5]], inc=True)
            _mm(T, B[4][0:64, 160:288], state_bf[0:32, :], QXI[:, cs], True, True, [stb_res, qxi_res[ch]], [BR[4]])
            yield
            ogb, ogr = og[ch % 2], og_res[ch % 2]
            j = i % 4
            T.op("dve", ("tensor_copy", dict(out=ogb2[0:64, :], in_=B[5][0:64, 0:128])), reads=[BR[5]], writes=[ogb2_res])
            T.op("dve", ("tensor_tensor", dict(out=ogb[0:64, j * 128:(j + 1) * 128], in0=ogb2[0:64, :], in1=B[4][0:64, 160:288], op=ALU.add)),
                 reads=[BR[4], ogb2_res], writes=[ogr[j]])
            T.op("dve", ("scalar_tensor_tensor", dict(out=state_f[0:32, :], in0=state_f[0:32, :], scalar=cd, in1=B[5][0:32, 128:192],
                                                      op0=ALU.mult, op1=ALU.add)), reads=[BR[5], st_res, cb_res], writes=[st_res])
            T.op("pool", ("tensor_copy", dict(out=state_bf[0:32, :], in_=state_f[0:32, :])), reads=[st_res], writes=[stb_res])
            yield
            if i % 4 == 3:
                cols = slice(ch * W, (ch + 1) * W)
                o64 = ogb[0:64, :]
                _mm(T, B[7][0:64, :], K.ones64[0:64, :], o64, True, True, [K.res] + ogr, [BR[7]])
                T.op("dve", ("tensor_tensor", dict(out=oc[0:64, :], in0=o64, in1=B[7][0:64, :], op=ALU.subtract)), reads=ogr + [BR[7]], writes=[gn_res])
                T.op("pool", ("tensor_tensor", dict(out=sq[0:64, :], in0=oc[0:64, :], in1=oc[0:64, :], op=ALU.mult)), reads=[gn_res], writes=[gn_res])
                yield
                _mm(T, B[7][0:64, :], K.ones64[0:64, :], sq[0:64, :], True, True, [K.res, gn_res], [BR[7]])
                T.op("act", ("activation", dict(out=rs[0:64, :], in_=B[7][0:64, :], func=AF.Ln, bias=K.eps[0:64, :], scale=1.0)), reads=[BR[7], K.res], writes=[gn_res])
                T.op("act", ("activation", dict(out=rs[0:64, :], in_=rs[0:64, :], func=AF.Exp, scale=-0.5)), reads=[gn_res], writes=[gn_res])
                T.op("dve", ("tensor_copy", dict(out=g0[0:64, :], in_=gT[:, cols])), reads=[g_res[ch]], writes=[gn_res])
                T.op("act", ("activation", dict(out=e1[0:64, :], in_=g0[0:64, :], func=AF.Exp, scale=-1.0)), reads=[gn_res], writes=[gn_res])
                yield
                T.op("dve", ("tensor_scalar", dict(out=e1[0:64, :], in0=e1[0:64, :], scalar1=1.0, scalar2=None, op0=ALU.add)), reads=[gn_res], writes=[gn_res])
                T.op("dve", ("reciprocal", dict(out=e1[0:64, :], in_=e1[0:64, :])), reads=[gn_res], writes=[gn_res])
                T.op("pool", ("tensor_tensor", dict(out=g0[0:64, :], in0=g0[0:64, :], in1=e1[0:64, :], op=ALU.mult)), reads=[gn_res], writes=[gn_res])
                T.op("dve", ("scalar_tensor_tensor", dict(out=t1[0:64, :], in0=oc[0:64, :], scalar=cb[0:64, CB_RETG:CB_RETG + 1], in1=rs[0:64, :],
                                                          op0=ALU.mult, op1=ALU.mult)), reads=[gn_res, cb_res], writes=[gn_res])
                rsb, rsr = rstage[ch % 2], rst_res[ch % 2]
                T.op("pool", ("tensor_tensor", dict(out=rsb[0:64, :], in0=t1[0:64, :], in1=g0[0:64, :], op=ALU.mult)), reads=[gn_res], writes=[rsr])
                T.dma("sp", ("dma_start", dict(out=ocat_out[128:192, OPAD + ch * W:OPAD + (ch + 1) * W], in_=rsb[0:64, :])),
                      reads=[rsr], writes=[ocat_res])
                yield

    rg = ret_gen()
    steps = [(qi, kt) for qi in range(S // QC) for kt in range(2 * qi + 2)]

    def qk(s):
        qi, kt = steps[s]
        sb, sr = B[s % 2], BR[s % 2]
        qs = slice(qi * QC, (qi + 1) * QC)
        ks = slice(kt * 128, (kt + 1) * 128)
        rd = [q_res[qi // 2], q1_res[qi // 2], k_res[kt // 4], qz_res]
        _mm(T, sb[:, :], kT[:, ks], qcat[:, qi, :, :].rearrange("p m n -> p (m n)"), True, True, rd, [sr], inc=True)

    qk(0)
    for s, (qi, kt) in enumerate(steps):
        nk = 2 * qi + 2
        if s + 1 < len(steps):
            qk(s + 1)
        sb, sr = B[s % 2], BR[s % 2]
        P, Pr = Pb[s % NP], P_res[s % NP]
        T.op("act", ("activation", dict(out=P, in_=sb[:, :], func=AF.Exp, scale=0.125)), reads=[sr], writes=[Pr])
        if kt >= 2 * qi:
            mk = masks[kt - 2 * qi]
            T.op("dve", ("tensor_tensor", dict(out=P, in0=P, in1=mk, op=ALU.mult)), reads=[Pr, cbb_res], writes=[Pr])
        _mm(T, B[2][:, :], VV[:, kt, 0:128], P, kt == 0, kt == nk - 1, [vv_res[kt // 4], Pr], [BR[2]], inc=True)
        ai = 1 if kt % 4 == 0 else 0
        aeng = "pool" if ai == 1 else "dve"
        if kt < 2 and kt == (0 if ai == 1 else 1):
            T.op(aeng, ("tensor_copy", dict(out=Lacc[ai], in_=P)), reads=[Pr], writes=[lacc_res[ai]])
        else:
            T.op(aeng, ("tensor_tensor", dict(out=Lacc[ai], in0=Lacc[ai], in1=P, op=ALU.add)), reads=[Pr, lacc_res[ai]], writes=[lacc_res[ai]])
        if s % 3 == 0:
            next(rg, None)
        if kt == nk - 1:
            _mm(T, B[3][:, :], K.ones32, Lacc[0], True, False, [K.res, lacc_res[0]], [BR[3]], inc=False)
            _mm(T, B[3][:, :], K.ones32, Lacc[1], False, True, [K.res, lacc_res[1]], [BR[3]], inc=True)
            T.op("dve", ("reciprocal", dict(out=Rb, in_=B[3][:, :])), reads=[BR[3]], writes=[ep_res])
            T.op("dve", ("tensor_tensor", dict(out=tb, in0=B[2][:, :], in1=Rb, op=ALU.mult)), reads=[BR[2], ep_res], writes=[ep_res])
            T.op("dve", ("scalar_tensor_tensor", dict(out=ob, in0=tb[:, QC:2 * QC], scalar=neg_lam, in1=tb[:, 0:QC],
                                                      op0=ALU.mult, op1=ALU.add)), reads=[ep_res, sc_res], writes=[ep_res])
            T.op("pool", ("tensor_tensor", dict(out=sqo, in0=ob, in1=ob, op=ALU.mult)), reads=[ep_res], writes=[ep_res])
            _mm(T, B[6][:, 0:QC], K.onesF, sqo, True, True, [K.res, ep_res], [BR[6]])
            T.op("act", ("activation", dict(out=l1, in_=B[6][:, 0:QC], func=AF.Ln, bias=K.eps, scale=1.0)), reads=[BR[6], K.res], writes=[ep_res])
            T.op("act", ("activation", dict(out=l1, in_=l1, func=AF.Exp, scale=-0.5)), reads=[ep_res], writes=[ep_res])
            st, h = ostage[(qi // 2) % 2], qi % 2
            T.op("dve", ("scalar_tensor_tensor", dict(out=st[:, h * QC:(h + 1) * QC], in0=ob, scalar=gda, in1=l1,
                                                      op0=ALU.mult, op1=ALU.mult)),
                 reads=[ep_res, sc_res], writes=[ost_res[(qi // 2) % 2][h]])
            if h == 1:
                g = qi // 2
                T.dma("sp", ("dma_start", dict(out=ocat_out[0:128, OPAD + g * W:OPAD + (g + 1) * W], in_=st)),
                      reads=ost_res[(qi // 2) % 2], writes=[ocat_res])
    for _ in rg:
        pass
    cx.release(m2)
    cx.release(m)


def _finish(cx):
    T = cx.T
    toks = [('c', n, e.count) for n, e in T.eng.items() if e.count > 0] + list(T.dma_last.values())
    T.wait_all("sp", toks)


def _load_x(cx, xin):
    T = cx.T
    xT = cx.alloc(8 * TCOL).rearrange("p (c n) -> p c n", c=8)
    nblk = NCH_C // BLK_C
    bw = CW * BLK_C
    xres = [Res(f"x{b}") for b in range(nblk)]
    xv = xin.rearrange("(c p) n -> p c n", p=128)
    for b in range(nblk):
        T.dma("sp", ("dma_start", dict(out=xT[:, :, b * bw:(b + 1) * bw], in_=xv[:, :, b * bw:(b + 1) * bw])), writes=[xres[b]])
    return xT, xres


def build_prog_A():
    nc = bass.Bass("TRN2", target_bir_lowering=False)
    with ExitStack() as st:
        cx = Ctx(nc, st)
        xin = cx.din("xT", [D, TCOL])
        g = cx.din("g_pre", [128, 8])
        hout = cx.dout("hT", [D, TOK], BF16)
        K = Consts(cx)
        xT, xres = _load_x(cx, xin)
        gains, g_res = load_gains(cx, K, g, 8)
        emit_phase_A(cx, K, xT, xres, gains, g_res, hout, Res("hT_out"))
        _finish(cx)
        cx.T.build(nc, st)
    return nc


def build_prog_B(parts="123e"):
    nc = bass.Bass("TRN2", target_bir_lowering=False)
    with ExitStack() as st:
        cx = Ctx(nc, st)
        hT_full = cx.din("hT_full", [NR, D, TOK], BF16)
        wh = cx.din("wh", [D, 1024])
        rot = cx.din("rot_tab", [16, 128, 2048])
        cb = cx.din("cb", [128, CB_N])
        cbb = cx.din("cbb", [128, CBB_N])
        oc = cx.dout("ocat", [256, OPAD + S], BF16)
        K = Consts(cx)
        emit_phase_B(cx, K, hT_full, Res("h_in"), wh, rot, cb, cbb, oc, Res("oc_out"), parts=parts)
        _finish(cx)
        cx.T.build(nc, st)
    return nc


def build_prog_CA():
    nc = bass.Bass("TRN2", target_bir_lowering=False)
    with ExitStack() as st:
        cx = Ctx(nc, st)
        xin = cx.din("xT", [D, TCOL])
        oc = cx.din("oc", [D, TCOL], BF16)
        wout = cx.din("wout", [8, 128, 8, 128])
        wup = cx.din("wup", [NPAIR, 128, 8, 256])
        wdown = cx.din("wdown", [8, 128, NPAIR, 128])
        cw = cx.din("cw", [128, NPAIR * 8])
        gains = cx.din("gains", [128, 24])
        mask = cx.din("mask", [128, 8])
        gnext = cx.din("g_pre", [128, 8])
        xout = cx.dout("xT_out", [D, TCOL])
        hout = cx.dout("hT", [D, TOK], BF16)
        K = Consts(cx)
        xT, xres = _load_x(cx, xin)
        emit_phase_C(cx, K, xT, xres, oc.rearrange("(k p) n -> p k n", p=128), Res("oc_in"), wout, wup, wdown, cw, gains, mask)
        T = cx.T
        xo = xout.rearrange("(c p) n -> p c n", p=128)
        bw = CW * BLK_C
        xo_res = Res("x_out")
        for b in range(NCH_C // BLK_C):
            T.dma("sp", ("dma_start", dict(out=xo[:, :, b * bw:(b + 1) * bw], in_=xT[:, :, b * bw:(b + 1) * bw])),
                  reads=[xres[b]], writes=[xo_res])
        g2, g2_res = load_gains(cx, K, gnext, 8)
        emit_phase_A(cx, K, xT, xres, g2, g2_res, hout, Res("hT_out"))
        _finish(cx)
        cx.T.build(nc, st)
    return nc


def _pc(v):
    return np.ascontiguousarray(np.asarray(v, np.float32).reshape(-1, 128).T)


def _rot_tables():
    pos = np.arange(S, dtype=np.float32)
    inv_da = (np.float32(500000.0) ** (-np.arange(0, 16, 2, dtype=np.float32) / np.float32(16))).astype(np.float32)
    inv_r = (np.float32(10000.0) ** (-np.arange(0, 32, 2, dtype=np.float32) / np.float32(32))).astype(np.float32)
    a_da = (pos[:, None] * inv_da[None, :]).astype(np.float32)
    a_r = (pos[:, None] * inv_r[None, :]).astype(np.float32)
    cda, sda = np.cos(a_da).astype(np.float32), np.sin(a_da).astype(np.float32)
    cr, sr = np.cos(a_r).astype(np.float32), np.sin(a_r).astype(np.float32)
    tab = np.zeros((128, 4, S), np.float32)
    tab[:, 0, :] = 1.0
    for base in (0, 64):
        for d in range(16):
            tab[base + d, 0, :] = cda[:, d % 8]
            tab[base + d, 1, :] = (-sda[:, d] if d < 8 else sda[:, d - 8])
    sc = np.float32(32.0 ** -0.5)
    for d in range(32):
        c = cr[:, d % 16]
        s_ = (-sr[:, d] if d < 16 else sr[:, d - 16])
        tab[d, 2, :] = c
        tab[d, 3, :] = s_
        tab[32 + d, 2, :] = c * sc
        tab[32 + d, 3, :] = s_ * sc
    t = tab.reshape(128, 4, 16, 512).transpose(2, 0, 1, 3).reshape(16, 128, 2048)
    return np.ascontiguousarray(t)


def _head_cols(h):
    def q_da(m, d):
        return 0 + h * 128 + m * 64 + d

    def k_da(m, d):
        return 512 + h * 128 + m * 64 + d

    def partner_da(d):
        return d + 8 if d < 8 else (d - 8 if d < 16 else d)

    def partner_r(d):
        return d + 16 if d < 16 else d - 16
    cols = []
    for f in (q_da, k_da):
        cols += [f(m, d) for m in range(2) for d in range(64)]
        cols += [f(m, partner_da(d)) for m in range(2) for d in range(64)]
    qr = [1536 + h * 32 + d for d in range(32)]
    kr = [1664 + h * 32 + d for d in range(32)]
    gr = [2048 + h * 64 + d for d in range(64)]
    cols += qr + kr + gr
    cols += [1536 + h * 32 + partner_r(d) for d in range(32)] + [1664 + h * 32 + partner_r(d) for d in range(32)] + gr
    cols += [1024 + h * 128 + d for d in range(128)] + [1792 + h * 64 + d for d in range(64)] + [2304 + h * 64 + d for d in range(64)]
    assert len(cols) == 1024
    return np.array(cols)


def _consts_B(h, l, diff_subln, ret_norm, pool_scale, pool_w, lq1, lk1, lq2, lk2):
    cb = np.zeros((128, CB_N), np.float32)
    gam = 1.0 - 2.0 ** (-5.0 - h)
    lg = np.float32(np.log(np.float32(gam)))
    idx = np.arange(128, dtype=np.float32)
    diff = idx[None, :] - idx[:, None]
    dec = np.where(diff >= 0, np.exp(np.maximum(diff, 0) * lg), 0.0).astype(np.float32)
    cb[:, CB_DECAY:CB_DECAY + 128] = dec
    cb[:, CB_ZETA] = np.exp((127.0 - idx) * lg)
    cb[:, CB_CD] = np.exp(np.float32(128.0) * lg)
    lam_init = 0.8 - 0.6 * math.exp(-0.3 * l)
    cb[:, CB_LAMI] = lam_init
    cb[:, CB_OML] = 1.0 - lam_init
    cb[:, CB_SUBLN] = diff_subln
    cb[0:64, CB_PSCALE] = pool_scale[h * 64:(h + 1) * 64]
    cb[0:64, CB_RETG] = ret_norm[h * 64:(h + 1) * 64]
    xi = np.exp((idx + 1.0) * lg).astype(np.float32)
    cb[:, CB_XI:CB_XI + 512] = np.tile(xi, 4)[None, :]
    w = (2, 4, 8, 16)[h]
    s_ = np.arange(128)[:, None]
    t_ = np.arange(128)[None, :]
    band = ((t_ - s_ >= 0) & (t_ - s_ < w)).astype(np.float32)
    cb[:, CB_BCUR:CB_BCUR + 128] = band / w - np.eye(128, dtype=np.float32)
    cb[:, CB_BPREV:CB_BPREV + 128] = ((t_ - s_ + 128 >= 0) & (t_ - s_ + 128 < w)).astype(np.float32) / w
    cb[:, CB_BCUR0:CB_BCUR0 + 128] = band / np.minimum(t_ + 1.0, float(w)) - np.eye(128, dtype=np.float32)
    cb[:, CB_LAMV:CB_LAMV + 256] = np.concatenate([lq1, lk1, lq2, lk2])[None, :]
    cbb = np.zeros((128, CBB_N), np.float32)
    cbb[0:32, CBB_I32:CBB_I32 + 32] = np.eye(32, dtype=np.float32)
    j = np.arange(256)[None, :]
    k = np.arange(128)[:, None]
    m0 = (j >= k).astype(np.float32)
    m1 = (j >= k + 128).astype(np.float32)
    cbb[:, CBB_MASK:CBB_MASK + 512] = np.concatenate([m0, m0], axis=1)
    cbb[:, CBB_MASK + 512:CBB_MASK + 1024] = np.concatenate([m1, m1], axis=1)
    cbb[0:64, CBB_PW:CBB_PW + 64] = pool_w[h]
    return cb, cbb


def _wout_rows():
    rows = []
    for r in range(NR):
        rows += [r * 128 + d for d in range(128)] + [512 + r * 64 + d for d in range(64)] + [768 + r * 64 + d for d in range(64)]
    return np.array(rows)


def _prep_layer(inp, l):
    P = {}
    w_in = np.asarray(inp["w_in"][l], np.float32)
    P["wh"] = [np.ascontiguousarray(w_in[:, _head_cols(h)]) for h in range(NR)]
    P["cB"] = [_consts_B(h, l, inp["diff_subln"][l], inp["ret_norm"][l], inp["pool_scale"][l], inp["pool_w"][l],
                         inp["lambda_q1"][l], inp["lambda_k1"][l], inp["lambda_q2"][l], inp["lambda_k2"][l]) for h in range(NR)]
    wo = np.asarray(inp["w_out"][l], np.float32)[_wout_rows(), :]
    P["wout"] = np.ascontiguousarray(wo.reshape(8, 128, 8, 128).transpose(2, 1, 0, 3))
    wu = np.asarray(inp["w_up"][l], np.float32)
    g = wu[:, :DFF].reshape(8, 128, NPAIR, 128)
    v = wu[:, DFF:].reshape(8, 128, NPAIR, 128)
    P["wup"] = np.ascontiguousarray(np.concatenate([g, v], axis=3).transpose(2, 1, 0, 3))
    wd = np.asarray(inp["w_down"][l], np.float32)
    P["wdown"] = np.ascontiguousarray(wd.reshape(NPAIR, 128, 8, 128).transpose(2, 1, 0, 3))
    cwl = np.asarray(inp["conv_w"][l], np.float32)
    cbl = np.asarray(inp["conv_b"][l], np.float32)
    cw = np.zeros((128, NPAIR, 2, 4), np.float32)
    for gv in range(2):
        blk = cwl[:, gv * DFF:(gv + 1) * DFF].reshape(3, NPAIR, 128)
        cw[:, :, gv, 0:3] = blk.transpose(2, 1, 0)
        cw[:, :, gv, 3] = cbl[gv * DFF:(gv + 1) * DFF].reshape(NPAIR, 128).T
    P["cw"] = np.ascontiguousarray(cw.reshape(128, NPAIR * 8))
    P["gains"] = np.ascontiguousarray(np.concatenate([_pc(inp["norm_mix_post"][l]), _pc(inp["norm_mlp_pre"][l]),
                                                      _pc(inp["norm_mlp_post"][l])], axis=1))
    P["g_pre"] = _pc(inp["norm_mix_pre"][l])
    return P


def _x_shards(x):
    out = []
    for b in range(NB):
        for r in range(NR):
            xt = np.zeros((D, TCOL), np.float32)
            lo = r * TOK - HALO
            if lo < 0:
                xt[:, HALO:] = x[b, 0:TOK, :].T
            else:
                xt[:] = x[b, lo:lo + TCOL, :].T
            out.append(xt)
    return out


_PROGS = {}


def _prog(name):
    if name not in _PROGS:
        _PROGS[name] = {"A": build_prog_A, "B": build_prog_B, "CA": build_prog_CA}[name]()
    return _PROGS[name]


def kernel(**inp):
    import ml_dtypes
    inp = {k: np.asarray(v) for k, v in inp.items()}
    x = inp["x"].astype(np.float32, copy=False)
    cores = list(range(8))
    xs = _x_shards(x)
    masks = [np.full((128, 8), 0.0 if (c % NR) == 0 else 1.0, np.float32) for c in cores]
    rot = _rot_tables()
    layers = [_prep_layer(inp, l) for l in range(DEPTH)]
    res = run_bass_kernel_spmd(_prog("A"), [{"xT": xs[c], "g_pre": layers[0]["g_pre"]} for c in cores], core_ids=cores)
    hT = [res.results[c]["hT"] for c in cores]
    for l in range(DEPTH):
        L = layers[l]
        ins = []
        for c in cores:
            b, r = divmod(c, NR)
            hfull = np.ascontiguousarray(np.stack([hT[b * NR + k] for k in range(NR)], axis=0))
            ins.append({"hT_full": hfull, "wh": L["wh"][r], "rot_tab": rot, "cb": L["cB"][r][0], "cbb": L["cB"][r][1]})
        res = run_bass_kernel_spmd(_prog("B"), ins, core_ids=cores)
        oc = [res.results[c]["ocat"] for c in cores]
        ins = []
        gnext = layers[l + 1]["g_pre"] if l + 1 < DEPTH else L["g_pre"]
        for c in cores:
            b, r = divmod(c, NR)
            ocs = np.ascontiguousarray(np.concatenate([oc[b * NR + k][:, OPAD - HALO + r * TOK:OPAD - HALO + r * TOK + TCOL] for k in range(NR)], axis=0))
            ins.append({"xT": xs[c], "oc": ocs, "wout": L["wout"], "wup": L["wup"], "wdown": L["wdown"], "cw": L["cw"],
                        "gains": L["gains"], "mask": masks[c], "g_pre": gnext})
        res = run_bass_kernel_spmd(_prog("CA"), ins, core_ids=cores)
        xs = [res.results[c]["xT_out"] for c in cores]
        hT = [res.results[c]["hT"] for c in cores]
    out = np.zeros((NB, S, D), np.float32)
    for c in cores:
        b, r = divmod(c, NR)
        out[b, r * TOK:(r + 1) * TOK, :] = xs[c][:, HALO:].T
    return out
```

```python
import math
from contextlib import ExitStack
import numpy as np
import concourse.bass as bass
import concourse.mybir as mybir
from concourse.bass_utils import run_bass_kernel_spmd

F32 = mybir.dt.float32
BF16 = mybir.dt.bfloat16
AF = mybir.ActivationFunctionType
ALU = mybir.AluOpType

D = 1024
S = 8192
NB = 2
DEPTH = 2
NR = 4
TOK = S // NR
HALO = 4
TCOL = TOK + HALO
DFF = 2816
NPAIR = DFF // 128
EPS = 1e-6
CW = 342
NCH_C = TCOL // CW
BLK_C = 2
QC = 256
OPAD = 256


class Res:
    __slots__ = ("name", "w", "r", "excl")

    def __init__(self, name="", excl=False):
        self.name = name
        self.w = None
        self.r = {}
        self.excl = excl


class _Eng:
    def __init__(self, name):
        self.name = name
        self.ops = []
        self.count = 0
        self.seen = {}


class Tracker:
    NDSEM = 8

    def __init__(self):
        self.eng = {n: _Eng(n) for n in ("pe", "act", "dve", "pool", "sp")}
        self.dma_n = {}
        self.dma_last = {}

    def _need(self, eng, tok, waits):
        if tok is None:
            return
        kind, key, val = tok
        k = (kind, key)
        if eng.seen.get(k, 0) >= val:
            return
        eng.seen[k] = val
        waits[k] = max(waits.get(k, 0), val)

    def _deps(self, eng, reads, writes):
        waits = {}
        for b in reads:
            self._need(eng, b.w, waits)
            if b.excl:
                for t in b.r.values():
                    if not (t[0] == 'c' and t[1] == eng.name):
                        self._need(eng, t, waits)
        for b in writes:
            if b.w is not None and not (b.w[0] == 'c' and b.w[1] == eng.name):
                self._need(eng, b.w, waits)
            for t in b.r.values():
                if t[0] == 'c' and t[1] == eng.name:
                    continue
                self._need(eng, t, waits)
        return waits

    def _commit(self, tok, reads, writes):
        for b in reads:
            b.r[(tok[0], tok[1])] = tok
        for b in writes:
            b.w = tok
            b.r = {}

    def op(self, engname, fn, reads=(), writes=(), inc=True):
        eng = self.eng[engname]
        waits = self._deps(eng, reads, writes)
        if engname == "pe":
            waits.pop(('c', 'pe'), None)
        for k, v in waits.items():
            eng.ops.append(('w', k, v))
        if inc:
            eng.count += 1
            tok = ('c', engname, eng.count)
            eng.ops.append(('o', fn, ('c', engname), 1))
        else:
            tok = ('c', engname, eng.count + 1)
            eng.ops.append(('o', fn, None, 0))
        self._commit(tok, reads, writes)
        return tok

    def dma(self, qname, fn, reads=(), writes=()):
        eng = self.eng[qname]
        n = self.dma_n.get(qname, 0)
        self.dma_n[qname] = n + 1
        slot = n % self.NDSEM
        rnd = n // self.NDSEM
        semkey = (qname, slot)
        waits = self._deps(eng, reads, writes)
        if rnd > 0:
            self._need(eng, ('d', semkey, 16 * rnd), waits)
        for k, v in waits.items():
            eng.ops.append(('w', k, v))
        tok = ('d', semkey, 16 * (rnd + 1))
        eng.ops.append(('o', fn, ('d', semkey), 16))
        self.dma_last[semkey] = tok
        self._commit(tok, reads, writes)
        return tok

    def wait_all(self, engname, toks):
        eng = self.eng[engname]
        waits = {}
        for t in toks:
            self._need(eng, t, waits)
        for k, v in waits.items():
            eng.ops.append(('w', k, v))

    def barrier(self):
        toks = [('c', n, e.count) for n, e in self.eng.items() if e.count > 0]
        toks += list(self.dma_last.values())
        for n in self.eng:
            self.wait_all(n, toks)

    def simulate(self):
        sem = {}
        pc = {n: 0 for n in self.eng}
        progress = True
        while progress:
            progress = False
            for n, e in self.eng.items():
                while pc[n] < len(e.ops):
                    o = e.ops[pc[n]]
                    if o[0] == 'w':
                        if sem.get(o[1], 0) >= o[2]:
                            pc[n] += 1
                            progress = True
                        else:
                            break
                    else:
                        if o[2] is not None:
                            sem[o[2]] = sem.get(o[2], 0) + o[3]
                        pc[n] += 1
                        progress = True
        stuck = {n: (pc[n], len(e.ops), e.ops[pc[n]][:3] if pc[n] < len(e.ops) else None) for n, e in self.eng.items() if pc[n] < len(e.ops)}
        return stuck, sem

    def build(self, nc, stack):
        sems = {}

        def sem(k):
            if k not in sems:
                nm = "s_" + "_".join(str(x) for x in (k[1] if isinstance(k[1], tuple) else (k[1],)))
                sems[k] = stack.enter_context(nc.semaphore(nm))
            return sems[k]
        for e in self.eng.values():
            for o in e.ops:
                if o[0] == 'w':
                    sem(o[1])
                elif o[2] is not None:
                    sem(o[2])
        block = stack.enter_context(nc.Block())

        def replay(h, ops):
            for o in ops:
                if o[0] == 'w':
                    h.wait_ge(sem(o[1]), o[2])
                else:
                    fn = o[1]
                    ins = getattr(h, fn[0])(**fn[1]) if isinstance(fn, tuple) else fn(h)
                    if o[2] is not None:
                        ins.then_inc(sem(o[2]), o[3])
        E = self.eng

        @block.tensor
        def _(h):
            replay(h, E["pe"].ops)

        @block.scalar
        def _(h):
            replay(h, E["act"].ops)

        @block.vector
        def _(h):
            replay(h, E["dve"].ops)

        @block.gpsimd
        def _(h):
            replay(h, E["pool"].ops)

        @block.sync
        def _(h):
            replay(h, E["sp"].ops)


ARENA_WORDS = 52000


class Ctx:
    def __init__(self, nc, stack):
        self.nc = nc
        self.T = Tracker()
        self.arena = stack.enter_context(nc.sbuf_tensor("arena", [128, ARENA_WORDS], F32))
        self.off = 0
        self.banks = [stack.enter_context(nc.psum_tensor(f"bank{i}", [128, 512], F32)) for i in range(8)]
        self.bank_res = [Res(f"bank{i}", excl=True) for i in range(8)]
        self.dram = {}

    def alloc(self, n, dt=F32, p0=0, p1=128):
        if dt == BF16:
            w = (n + 1) // 2
            a = self.arena[p0:p1, self.off:self.off + w].bitcast(BF16)
            if n % 2:
                a = a[:, 0:n]
        else:
            w = n
            a = self.arena[p0:p1, self.off:self.off + w]
        self.off += w
        assert self.off <= ARENA_WORDS, f"arena overflow {self.off}"
        return a

    def alloc_at(self, off_words, n, dt=F32, p0=0, p1=128):
        if dt == BF16:
            w = (n + 1) // 2
            a = self.arena[p0:p1, off_words:off_words + w].bitcast(BF16)
        else:
            w = n
            a = self.arena[p0:p1, off_words:off_words + w]
        assert off_words + w <= ARENA_WORDS
        return a

    def mark(self):
        return self.off

    def release(self, m):
        self.T.barrier()
        self.off = m

    def din(self, name, shape, dt=F32):
        t = self.nc.dram_tensor(name, list(shape), dt, kind="ExternalInput").ap()
        self.dram[name] = t
        return t

    def dout(self, name, shape, dt=F32):
        t = self.nc.dram_tensor(name, list(shape), dt, kind="ExternalOutput").ap()
        self.dram[name] = t
        return t

    def dscratch(self, name, shape, dt=F32):
        t = self.nc.dram_tensor(name, list(shape), dt, kind="Internal").ap()
        self.dram[name] = t
        return t


def _mm(T, out, lhsT, rhs, start, stop, reads, writes, inc=True):
    T.op("pe", ("matmul", dict(out=out, lhsT=lhsT, rhs=rhs, start=start, stop=stop)), reads=reads, writes=writes, inc=inc)


class Consts:
    def __init__(self, cx):
        T = cx.T
        self.res = Res("consts")
        self.ones_bf = cx.alloc(128, BF16)
        self.onesF = cx.alloc(128)
        self.ones64 = cx.alloc(64)
        self.col = cx.alloc(8)
        T.op("pool", ("memset", dict(ap=self.ones_bf, constant=1.0)), writes=[self.res])
        T.op("pool", ("memset", dict(ap=self.onesF, constant=1.0 / 128.0)), writes=[self.res])
        T.op("pool", ("memset", dict(ap=self.ones64, constant=1.0 / 64.0)), writes=[self.res])
        vals = [EPS * D, EPS, 1.0, 32.0]
        for i, v in enumerate(vals):
            T.op("pool", ("memset", dict(ap=self.col[:, i:i + 1], constant=v)), writes=[self.res])
        self.eps_d = self.col[:, 0:1]
        self.eps = self.col[:, 1:2]
        self.one = self.col[:, 2:3]


def load_gains(cx, K, dram_g, n):
    T = cx.T
    g = cx.alloc(n)
    r = Res("gains")
    T.dma("sp", ("dma_start", dict(out=g, in_=dram_g)), writes=[r])
    T.op("dve", ("tensor_scalar", dict(out=g, in0=g, scalar1=32.0, scalar2=None, op0=ALU.mult)), reads=[r], writes=[r])
    return g, r


def xres_for(xres, c0, n):
    bw = CW * BLK_C
    return [xres[b] for b in range(len(xres)) if b * bw < c0 + n and (b + 1) * bw > c0]


def emit_phase_A(cx, K, xT, xres, g_ap, g_res, hT_out, hT_res):
    T = cx.T
    m = cx.mark()
    NCH = 4
    W = 512
    sq = [cx.alloc(W, BF16) for _ in range(2)]
    sq_res = [Res("A_sq0"), Res("A_sq1")]
    rstd = [cx.alloc(W) for _ in range(2)]
    rstd_res = [Res("A_rstd0"), Res("A_rstd1")]
    hbuf = [cx.alloc(8 * W, BF16).rearrange("p (c n) -> p c n", c=8) for _ in range(2)]
    hbuf_res = [[Res(f"A_h{i}_{c}") for c in range(8)] for i in range(2)]
    hview = hT_out.rearrange("(c p) n -> p c n", p=128)
    for ch in range(NCH):
        c0 = HALO + ch * W
        xr = xres_for(xres, c0, W)
        bank = cx.banks[ch % 2][:, 0:W]
        bres = cx.bank_res[ch % 2]
        for c in range(8):
            s = sq[c % 2]
            sr = sq_res[c % 2]
            T.op("act", ("activation", dict(out=s, in_=xT[:, c, c0:c0 + W], func=AF.Square)),
                 reads=xr, writes=[sr])
            _mm(T, bank, K.ones_bf, s, c == 0, c == 7, [sr, K.res], [bres])
        r = rstd[ch % 2]
        rr = rstd_res[ch % 2]
        T.op("act", ("activation", dict(out=r, in_=bank, func=AF.Sqrt, bias=K.eps_d, scale=1.0)),
             reads=[bres, K.res], writes=[rr])
        T.op("dve", ("reciprocal", dict(out=r, in_=r)), reads=[rr], writes=[rr])
        hb = hbuf[ch % 2]
        hr = hbuf_res[ch % 2]
        for c in range(8):
            T.op("dve", ("scalar_tensor_tensor", dict(
                out=hb[:, c, :], in0=xT[:, c, c0:c0 + W], scalar=g_ap[:, c:c + 1], in1=r,
                op0=ALU.mult, op1=ALU.mult)), reads=xr + [rr, g_res], writes=[hr[c]])
        T.dma("sp", ("dma_start", dict(out=hview[:, :, ch * W:(ch + 1) * W], in_=hb)),
              reads=hr, writes=[hT_res])
    cx.release(m)


def emit_phase_C(cx, K, xT, xres, oc_src, oc_res, wout, wup, wdown, cw_d, gains_d, mask_d):
    T = cx.T
    m = cx.mark()
    BW = CW * BLK_C
    NBLK = NCH_C // BLK_C
    wres = Res("C_wdram")
    cw = cx.alloc(NPAIR * 8).rearrange("p (j g k) -> p j g k", j=NPAIR, g=2)
    cw_res = Res("C_cw")
    T.dma("sp", ("dma_start", dict(out=cw, in_=cw_d.rearrange("p (j g k) -> p j g k", j=NPAIR, g=2))), writes=[cw_res])
    gains, g_res = load_gains(cx, K, gains_d, 24)
    msk = cx.alloc(8)
    msk_res = Res("C_mask")
    T.dma("sp", ("dma_start", dict(out=msk, in_=mask_d)), writes=[msk_res])
    ocb = cx.alloc(8 * BW, BF16).rearrange("p (c n) -> p c n", c=8)
    ocb_res = Res("C_ocb")
    ysb = cx.alloc(8 * BW).rearrange("p (c n) -> p c n", c=8)
    ysb_res = [[Res(f"C_ysb{d}_{k}") for k in range(BLK_C)] for d in range(8)]
    sqb = [cx.alloc(CW, BF16) for _ in range(2)]
    sqb_res = [Res("C_sq0"), Res("C_sq1")]
    rstd = [cx.alloc(CW) for _ in range(2)]
    rstd_res = [Res("C_rstd0"), Res("C_rstd1")]
    hb = cx.alloc(8 * (BW + 2), BF16).rearrange("p (c n) -> p c n", c=8)
    hb_res = [[Res(f"C_hb{c}_{k}") for k in range(BLK_C)] for c in range(8)]
    hprev = cx.alloc(8 * 2, BF16).rearrange("p (c n) -> p c n", c=8)
    hprev_res = Res("C_hprev")
    hb_pre_res = Res("C_hbpre")
    act = cx.alloc(NPAIR * BW, BF16).rearrange("p (j n) -> p j n", j=NPAIR)
    wup_s = [cx.alloc(8 * 256, BF16).rearrange("p (c n) -> p c n", c=8) for _ in range(3)]
    wup_res = [Res(f"C_wup{i}") for i in range(3)]
    wdn_s = [cx.alloc(NPAIR * 128, BF16).rearrange("p (k n) -> p k n", k=NPAIR) for _ in range(2)]
    wdn_res = [Res(f"C_wdn{i}") for i in range(2)]
    wo_s = [cx.alloc(8 * 128, BF16).rearrange("p (k n) -> p k n", k=8) for _ in range(2)]
    wo_res = [Res(f"C_wo{i}") for i in range(2)]
    tbuf = [cx.alloc(CW) for _ in range(4)]
    tbuf_res = [Res(f"C_t{i}") for i in range(4)]
    glb = [cx.alloc(CW) for _ in range(2)]
    glb_res = [Res(f"C_gl{i}") for i in range(2)]
    T.op("pool", ("memset", dict(ap=hprev, constant=0.0)), writes=[hprev_res])
    wst = [cx.alloc(8 * 256).rearrange("p (c n) -> p c n", c=8) for _ in range(2)]
    wst_res = [Res("C_wst0"), Res("C_wst1")]
    nld = [0]

    def load_wup(j):
        if j is None:
            return
        sti = nld[0] % 2
        nld[0] += 1
        T.dma("sp", ("dma_start", dict(out=wst[sti], in_=wup[j % NPAIR])), reads=[wres], writes=[wst_res[sti]])
        T.op("act", ("activation", dict(out=wup_s[j % 3], in_=wst[sti], func=AF.Copy)), reads=[wst_res[sti]], writes=[wup_res[j % 3]])

    nwo = [0]
    nwd = [0]
    nwu = [0]
    nt = [0]
    ngl = [0]
    nsq = [0]

    def norm_and_residual(blk, k, gidx):
        cc = slice(k * CW, (k + 1) * CW)
        xc = slice(blk * BW + k * CW, blk * BW + (k + 1) * CW)
        bank = cx.banks[2][:, 0:CW]
        bres = cx.bank_res[2]
        for d in range(8):
            i = nsq[0] % 2
            nsq[0] += 1
            T.op("pool", ("tensor_tensor", dict(out=sqb[i], in0=ysb[:, d, cc], in1=ysb[:, d, cc], op=ALU.mult)),
                 reads=[ysb_res[d][k]], writes=[sqb_res[i]])
            _mm(T, bank, K.ones_bf, sqb[i], d == 0, d == 7, [sqb_res[i], K.res], [bres])
        r = rstd[k]
        rr = rstd_res[k]
        T.op("act", ("activation", dict(out=r, in_=bank, func=AF.Sqrt, bias=K.eps_d, scale=1.0)),
             reads=[bres, K.res], writes=[rr])
        T.op("dve", ("reciprocal", dict(out=r, in_=r)), reads=[rr], writes=[rr])
        for d in range(8):
            T.op("dve", ("scalar_tensor_tensor", dict(
                out=ysb[:, d, cc], in0=ysb[:, d, cc], scalar=gains[:, gidx * 8 + d:gidx * 8 + d + 1], in1=r,
                op0=ALU.mult, op1=ALU.mult)), reads=[ysb_res[d][k], rr, g_res], writes=[ysb_res[d][k]])
            T.op("dve", ("tensor_tensor", dict(out=xT[:, d, xc], in0=xT[:, d, xc], in1=ysb[:, d, cc], op=ALU.add)),
                 reads=[ysb_res[d][k], xres[blk]], writes=[xres[blk]])

    for blk in range(NBLK):
        b0 = blk * BW
        T.dma("sp", ("dma_start", dict(out=ocb, in_=oc_src[:, :, b0:b0 + BW])), reads=[oc_res], writes=[ocb_res])
        for d in range(8):
            si = nwo[0] % 2
            nwo[0] += 1
            T.dma("pool", ("dma_start", dict(out=wo_s[si], in_=wout[d])), reads=[wres], writes=[wo_res[si]])
            for k in range(BLK_C):
                bank = cx.banks[(d * BLK_C + k) % 2][:, 0:CW]
                bres = cx.bank_res[(d * BLK_C + k) % 2]
                for kt in range(8):
                    _mm(T, bank, wo_s[si][:, kt, :], ocb[:, kt, k * CW:(k + 1) * CW], kt == 0, kt == 7,
                        [wo_res[si], ocb_res], [bres], inc=(kt == 7))
                T.op("act", ("activation", dict(out=ysb[:, d, k * CW:(k + 1) * CW], in_=bank, func=AF.Copy)),
                     reads=[bres], writes=[ysb_res[d][k]])
        for k in range(BLK_C):
            norm_and_residual(blk, k, 0)
        T.op("pool", ("tensor_copy", dict(out=hb[:, :, 0:2], in_=hprev)), reads=[hprev_res],
             writes=[hb_pre_res])
        for k in range(BLK_C):
            xc = slice(b0 + k * CW, b0 + (k + 1) * CW)
            bank = cx.banks[2][:, 0:CW]
            bres = cx.bank_res[2]
            for d in range(8):
                i = nsq[0] % 2
                nsq[0] += 1
                T.op("pool", ("tensor_tensor", dict(out=sqb[i], in0=xT[:, d, xc], in1=xT[:, d, xc], op=ALU.mult)),
                     reads=[xres[blk]], writes=[sqb_res[i]])
                _mm(T, bank, K.ones_bf, sqb[i], d == 0, d == 7, [sqb_res[i], K.res], [bres])
            r = rstd[k]
            rr = rstd_res[k]
            T.op("act", ("activation", dict(out=r, in_=bank, func=AF.Sqrt, bias=K.eps_d, scale=1.0)),
                 reads=[bres, K.res], writes=[rr])
            T.op("dve", ("reciprocal", dict(out=r, in_=r)), reads=[rr], writes=[rr])
            for d in range(8):
                T.op("dve", ("scalar_tensor_tensor", dict(
                    out=hb[:, d, 2 + k * CW:2 + (k + 1) * CW], in0=xT[:, d, xc], scalar=gains[:, 8 + d:9 + d], in1=r,
                    op0=ALU.mult, op1=ALU.mult)), reads=[xres[blk], rr, g_res], writes=[hb_res[d][k]])
            if blk == 0 and k == 0:
                for d in range(8):
                    T.op("pool", ("tensor_scalar", dict(out=hb[:, d, 2:2 + HALO], in0=hb[:, d, 2:2 + HALO],
                                                                 scalar1=msk[:, 0:1], scalar2=None, op0=ALU.mult)),
                         reads=[hb_res[d][0], msk_res], writes=[hb_res[d][0]])
        allhb = [hb_res[d][k] for d in range(8) for k in range(BLK_C)]
        T.op("pool", ("tensor_copy", dict(out=hprev, in_=hb[:, :, BW:BW + 2])), reads=allhb, writes=[hprev_res])
        act_res = [[Res(f"C_act{j}_{k}") for k in range(BLK_C)] for j in range(NPAIR)]
        for j in range(NPAIR):
            gj = blk * NPAIR + j
            if gj == 0:
                load_wup(0)
            si = gj % 3
            for k in range(BLK_C):
                if k == 1:
                    load_wup(gj + 1 if gj + 1 < NBLK * NPAIR else None)
                hc = slice(k * CW, k * CW + CW + 2)
                rd = [hb_res[d][k] for d in range(8)] + ([hb_pre_res] if k == 0 else [hb_res[d][k - 1] for d in range(8)])
                pb = 3 + 2 * ((j * BLK_C + k) % 2)
                tt = []
                for gv in range(2):
                    bank = cx.banks[pb + gv][:, 0:CW + 2]
                    bres = cx.bank_res[pb + gv]
                    for c in range(8):
                        _mm(T, bank, wup_s[si][:, c, gv * 128:(gv + 1) * 128], hb[:, c, hc], c == 0, c == 7,
                            [wup_res[si]] + rd, [bres], inc=(c == 7))
                    ti = nt[0] % 4
                    nt[0] += 1
                    t = tbuf[ti]
                    tr = tbuf_res[ti]
                    T.op("act", ("activation", dict(
                        out=t, in_=bank[:, 2:CW + 2], func=AF.Identity, bias=cw[:, j, gv, 3:4], scale=cw[:, j, gv, 2:3])),
                        reads=[bres, cw_res], writes=[tr])
                    T.op("dve", ("scalar_tensor_tensor", dict(
                        out=t, in0=bank[:, 1:CW + 1], scalar=cw[:, j, gv, 1:2], in1=t, op0=ALU.mult, op1=ALU.add)),
                        reads=[bres, cw_res, tr], writes=[tr])
                    T.op("dve", ("scalar_tensor_tensor", dict(
                        out=t, in0=bank[:, 0:CW], scalar=cw[:, j, gv, 0:1], in1=t, op0=ALU.mult, op1=ALU.add)),
                        reads=[bres, cw_res, tr], writes=[tr])
                    tt.append((t, tr))
                gi = ngl[0] % 2
                ngl[0] += 1
                T.op("act", ("activation", dict(out=glb[gi], in_=tt[0][0], func=AF.Gelu_apprx_tanh)),
                     reads=[tt[0][1]], writes=[glb_res[gi]])
                T.op("pool", ("tensor_tensor", dict(
                    out=act[:, j, k * CW:(k + 1) * CW], in0=glb[gi], in1=tt[1][0], op=ALU.mult)),
                    reads=[glb_res[gi], tt[1][1]], writes=[act_res[j][k]])
        for d in range(8):
            si = nwd[0] % 2
            nwd[0] += 1
            T.dma("pool", ("dma_start", dict(out=wdn_s[si], in_=wdown[d])), reads=[wres], writes=[wdn_res[si]])
            for k in range(BLK_C):
                bank = cx.banks[(d * BLK_C + k) % 2][:, 0:CW]
                bres = cx.bank_res[(d * BLK_C + k) % 2]
                for j in range(NPAIR):
                    _mm(T, bank, wdn_s[si][:, j, :], act[:, j, k * CW:(k + 1) * CW], j == 0, j == NPAIR - 1,
                        [wdn_res[si], act_res[j][k]], [bres], inc=(j == NPAIR - 1))
                T.op("act", ("activation", dict(out=ysb[:, d, k * CW:(k + 1) * CW], in_=bank, func=AF.Copy)),
                     reads=[bres], writes=[ysb_res[d][k]])
        for k in range(BLK_C):
            norm_and_residual(blk, k, 2)
    cx.release(m)


CB_DECAY = 0
CB_ZETA = 128
CB_CD = 129
CB_LAMI = 130
CB_OML = 131
CB_SUBLN = 132
CB_PSCALE = 133
CB_RETG = 134
CB_XI = 135
CB_BCUR = CB_XI + 512
CB_BPREV = CB_BCUR + 128
CB_BCUR0 = CB_BPREV + 128
CB_LAMV = CB_BCUR0 + 128
CB_N = CB_LAMV + 256
CBB_I32 = 0
CBB_MASK = 32
CBB_PW = 32 + 1024
CBB_N = CBB_PW + 64


def emit_phase_B(cx, K, hT_full, h_res, wh_d, rot_tab, cb_d, cbb_d, ocat_out, ocat_res, parts="123e"):
    T = cx.T
    m = cx.mark()
    NCH = 16
    W = 512
    cres = Res("B_cdram")
    cb = cx.alloc(CB_N)
    cb_res = Res("B_cb")
    T.dma("sp", ("dma_start", dict(out=cb, in_=cb_d)), reads=[cres], writes=[cb_res])
    cbb = cx.alloc(CBB_N, BF16)
    cbb_res = Res("B_cbb")
    T.dma("pool", ("dma_start", dict(out=cbb, in_=cbb_d)), reads=[cres], writes=[cbb_res])
    decayT = cb[:, CB_DECAY:CB_DECAY + 128]
    zeta = cb[:, CB_ZETA:CB_ZETA + 1]
    cd = cb[0:32, CB_CD:CB_CD + 1]
    xi4 = cb[0:32, CB_XI:CB_XI + 512]
    bcur = cb[:, CB_BCUR:CB_BCUR + 128]
    bprev = cb[:, CB_BPREV:CB_BPREV + 128]
    bcur0 = cb[:, CB_BCUR0:CB_BCUR0 + 128]
    i32 = cbb[0:32, CBB_I32:CBB_I32 + 32]
    masks = [cbb[:, CBB_MASK + i * 512:CBB_MASK + (i + 1) * 512] for i in range(2)]
    pw = cbb[0:64, CBB_PW:CBB_PW + 64]

    qT0 = cx.alloc(S, BF16)
    qT1 = cx.alloc(S, BF16)
    kT = cx.alloc(S, BF16)
    qz_res = Res("B_qzero")
    T.op("pool", ("memset", dict(ap=qT0[64:128, :], constant=0.0)), writes=[qz_res])
    T.op("pool", ("memset", dict(ap=qT1[0:64, :], constant=0.0)), writes=[qz_res])
    VV = cx.alloc(64 * 192, BF16).rearrange("p (k n) -> p k n", k=64)
    goff = cx.off
    cx.off += 8192
    assert cx.off <= ARENA_WORDS
    gT = cx.alloc_at(goff, 8192, F32, 64, 128)
    RQ = cx.alloc_at(goff, 8192, BF16, 0, 32)
    RK = cx.alloc_at(goff + 4096, 8192, BF16, 0, 32)
    QXI = cx.alloc(S, BF16, 0, 32)
    q_res = [Res(f"B_q{i}") for i in range(NCH)]
    q1_res = [Res(f"B_q1{i}") for i in range(NCH)]
    k_res = [Res(f"B_k{i}") for i in range(NCH)]
    vv_res = [Res(f"B_vv{i}") for i in range(NCH)]
    g_res = [Res(f"B_g{i}") for i in range(NCH)]
    rq_res = [Res(f"B_rq{i}") for i in range(NCH)]
    rk_res = [Res(f"B_rk{i}") for i in range(NCH)]
    qxi_res = [Res(f"B_qxi{i}") for i in range(NCH)]

    zp = cx.alloc(OPAD, BF16)
    zp_res = Res("B_zp")
    T.op("pool", ("memset", dict(ap=zp, constant=0.0)), writes=[zp_res])
    for h in range(2):
        T.dma("sp", ("dma_start", dict(out=ocat_out[h * 128:(h + 1) * 128, 0:OPAD], in_=zp)), reads=[zp_res], writes=[ocat_res])

    m1 = cx.mark()
    wh = cx.alloc(8 * 1024, BF16).rearrange("p (c n) -> p c n", c=8)
    wh_res = [Res(f"B_wh{c}") for c in range(8)]
    whv = wh_d.rearrange("(c p) n -> p c n", p=128)
    for c in range(8):
        T.dma("pool", ("dma_start", dict(out=wh[:, c, :], in_=whv[:, c, :])), reads=[cres], writes=[wh_res[c]])
    hcb = [cx.alloc(8 * W, BF16).rearrange("p (c n) -> p c n", c=8) for _ in range(2)]
    hcb_res = [Res("B_hc0"), Res("B_hc1")]
    tabb = [cx.alloc(4 * W).rearrange("p (s n) -> p s n", s=4) for _ in range(2)]
    tabb_res = [Res("B_tab0"), Res("B_tab1")]
    tAb = [cx.alloc(W) for _ in range(2)]
    tBb = [cx.alloc(W) for _ in range(2)]
    tA_res = [Res("B_tA0"), Res("B_tA1")]
    tB_res = [Res("B_tB0"), Res("B_tB1")]
    ucur = [cx.alloc(4 * 64).rearrange("p (t n) -> p t n", t=4) for _ in range(2)]
    ucur_res = [[Res(f"B_u{i}_{t}") for t in range(4)] for i in range(2)]
    pooled = cx.alloc(W, BF16)
    pooled_res = Res("B_pooled")
    pstage = [cx.alloc(W, BF16) for _ in range(2)]
    pstage_res = [Res("B_ps0"), Res("B_ps1")]
    nrot = [0]

    def rotary(ps, ps_sw, bres, bres_sw, tab, tab_r, cslot, sslot, p0, p1, outs):
        i = nrot[0] % 2
        nrot[0] += 1
        tA, tB = tAb[i], tBb[i]
        T.op("dve", ("tensor_tensor", dict(out=tA[p0:p1, :], in0=ps_sw[p0:p1, :], in1=tab[p0:p1, sslot, :], op=ALU.mult)),
             reads=[bres_sw, tab_r], writes=[tA_res[i]])
        T.op("dve", ("tensor_tensor", dict(out=tB[p0:p1, :], in0=ps[p0:p1, :], in1=tab[p0:p1, cslot, :], op=ALU.mult)),
             reads=[bres, tab_r], writes=[tB_res[i]])
        for dst, s0, s1, r in outs:
            T.op("pool", ("tensor_tensor", dict(out=dst, in0=tA[s0:s1, :], in1=tB[s0:s1, :], op=ALU.add)),
                 reads=[tA_res[i], tB_res[i]], writes=[r])

    for ch in range(NCH):
        rank, lc = ch // 4, (ch % 4) * W
        cols = slice(ch * W, (ch + 1) * W)
        hc, hcr = hcb[ch % 2], hcb_res[ch % 2]
        tab, tabr = tabb[ch % 2], tabb_res[ch % 2]
        hv = hT_full[rank].rearrange("(c p) n -> p c n", p=128)
        T.dma("sp", ("dma_start", dict(out=hc, in_=hv[:, :, lc:lc + W])), reads=[h_res], writes=[hcr])
        T.dma("sp", ("dma_start", dict(out=tab, in_=rot_tab[ch].rearrange("p (s n) -> p s n", s=4))),
              reads=[cres], writes=[tabr])
        for ti in range(6):
            bank = cx.banks[ti][:, :]
            for c in range(8):
                _mm(T, bank, wh[:, c, ti * 128:(ti + 1) * 128], hc[:, c, :], c == 0, c == 7,
                    [wh_res[c], hcr], [cx.bank_res[ti]], inc=(c == 7))
        B = cx.banks
        BR = cx.bank_res
        rotary(B[0], B[1], BR[0], BR[1], tab, tabr, 0, 1, 0, 128,
               [(qT0[0:64, cols], 0, 64, q_res[ch]), (qT1[64:128, cols], 64, 128, q1_res[ch])])
        rotary(B[2], B[3], BR[2], BR[3], tab, tabr, 0, 1, 0, 128, [(kT[:, cols], 0, 128, k_res[ch])])
        T.op("act", ("activation", dict(out=gT[:, cols], in_=B[4][64:128, :], func=AF.Copy)),
             reads=[BR[4]], writes=[g_res[ch]])
        rotary(B[4], B[5], BR[4], BR[5], tab, tabr, 2, 3, 0, 64,
               [(RQ[:, cols], 0, 32, rq_res[ch]), (RK[:, cols], 32, 64, rk_res[ch])])
        T.op("pool", ("tensor_tensor", dict(out=QXI[:, cols], in0=RQ[:, cols], in1=xi4, op=ALU.mult)),
             reads=[rq_res[ch], cb_res], writes=[qxi_res[ch]])
        uc, ucr = ucur[ch % 2], ucur_res[ch % 2]
        for tt in range(4):
            bank = B[6 + tt // 2][:, (tt % 2) * 256:(tt % 2) * 256 + 256]
            bres = BR[6 + tt // 2]
            for c in range(8):
                _mm(T, bank, hc[:, c, tt * 128:(tt + 1) * 128], wh[:, c, 768:1024], c == 0, c == 7,
                    [wh_res[c], hcr], [bres], inc=(c == 7))
            kt = ch * 4 + tt
            T.op("act", ("activation", dict(out=VV[:, kt, :], in_=bank[:, 0:192], func=AF.Copy)),
                 reads=[bres], writes=[vv_res[ch]])
            T.op("dve", ("tensor_copy", dict(out=uc[:, tt, :], in_=bank[:, 192:256])),
                 reads=[bres], writes=[ucr[tt]])
        for tt in range(4):
            tile = ch * 4 + tt
            out = B[0][0:64, tt * 128:(tt + 1) * 128]
            _mm(T, out, uc[:, tt, :], bcur0 if tile == 0 else bcur, True, tile == 0, [ucr[tt], cb_res], [BR[0]])
            if tile > 0:
                if tt > 0:
                    prev, prevr = uc[:, tt - 1, :], ucr[tt - 1]
                else:
                    prev, prevr = ucur[(ch - 1) % 2][:, 3, :], ucur_res[(ch - 1) % 2][3]
                _mm(T, out, prev, bprev, False, True, [prevr, cb_res], [BR[0]])
        T.op("act", ("activation", dict(out=pooled[0:64, :], in_=B[0][0:64, :], func=AF.Copy)), reads=[BR[0]], writes=[pooled_res])
        _mm(T, B[1][0:64, :], pw, pooled[0:64, :], True, True, [cbb_res, pooled_res], [BR[1]])
        pst, pstr = pstage[ch % 2], pstage_res[ch % 2]
        T.op("act", ("activation", dict(out=pst[0:64, :], in_=B[1][0:64, :], func=AF.Copy,
                                                      scale=cb[0:64, CB_PSCALE:CB_PSCALE + 1])),
             reads=[BR[1], cb_res], writes=[pstr])
        T.dma("sp", ("dma_start", dict(out=ocat_out[192:256, OPAD + ch * W:OPAD + (ch + 1) * W], in_=pst[0:64, :])),
              reads=[pstr], writes=[ocat_res])
    cx.release(m1)

    lw = cx.alloc(128)
    sc = cx.alloc(8)
    sc_res = Res("B_sc")
    lv = cb[:, CB_LAMV:CB_LAMV + 256]
    T.op("dve", ("tensor_tensor", dict(out=lw[:, 0:64], in0=lv[:, 0:64], in1=lv[:, 64:128], op=ALU.mult)), reads=[cb_res], writes=[sc_res])
    T.op("dve", ("tensor_tensor", dict(out=lw[:, 64:128], in0=lv[:, 128:192], in1=lv[:, 192:256], op=ALU.mult)), reads=[cb_res], writes=[sc_res])
    T.op("dve", ("reduce_sum", dict(out=sc[:, 0:1], in_=lw[:, 0:64], axis=mybir.AxisListType.X)), reads=[sc_res], writes=[sc_res])
    T.op("dve", ("reduce_sum", dict(out=sc[:, 1:2], in_=lw[:, 64:128], axis=mybir.AxisListType.X)), reads=[sc_res], writes=[sc_res])
    T.op("act", ("activation", dict(out=sc[:, 2:4], in_=sc[:, 0:2], func=AF.Exp)), reads=[sc_res], writes=[sc_res])
    T.op("dve", ("tensor_tensor", dict(out=sc[:, 4:5], in0=sc[:, 3:4], in1=sc[:, 2:3], op=ALU.subtract)), reads=[sc_res], writes=[sc_res])
    T.op("dve", ("tensor_tensor", dict(out=sc[:, 4:5], in0=sc[:, 4:5], in1=cb[:, CB_LAMI:CB_LAMI + 1], op=ALU.subtract)),
         reads=[sc_res, cb_res], writes=[sc_res])
    T.op("dve", ("tensor_tensor", dict(out=sc[:, 5:6], in0=cb[:, CB_SUBLN:CB_SUBLN + 1], in1=cb[:, CB_OML:CB_OML + 1], op=ALU.mult)),
         reads=[cb_res], writes=[sc_res])
    neg_lam = sc[:, 4:5]
    gda = sc[:, 5:6]

    m2 = cx.mark()
    Pb = [cx.alloc(W, BF16) for _ in range(3)]
    P_res = [Res(f"B_P{i}") for i in range(3)]
    Rb = cx.alloc(W)
    tb = cx.alloc(W)
    ob = cx.alloc(QC)
    sqo = cx.alloc(QC)
    l1 = cx.alloc(QC)
    ostage = [cx.alloc(W, BF16) for _ in range(2)]
    ost_res = [[Res(f"B_ost{i}_{h}") for h in range(2)] for i in range(2)]
    ep_res = Res("B_ep")
    innerM = [cx.alloc(128, BF16) for _ in range(2)]
    inner_res = [Res("B_in0"), Res("B_in1")]
    kz = [cx.alloc(32, BF16) for _ in range(2)]
    kz_res = [Res("B_kz0"), Res("B_kz1")]
    state_f = cx.alloc(64)
    state_bf = cx.alloc(64, BF16)
    ogb2 = cx.alloc(128)
    ogb2_res = Res("B_ogb2")
    st_res = Res("B_stf")
    stb_res = Res("B_stb")
    og = [cx.alloc(W) for _ in range(2)]
    og_res = [[Res(f"B_og{i}_{j}") for j in range(4)] for i in range(2)]
    oc = cx.alloc(W)
    sq = cx.alloc(W)
    rs = cx.alloc(W)
    g0 = cx.alloc(W)
    e1 = cx.alloc(W)
    t1 = cx.alloc(W)
    rstage = [cx.alloc(W, BF16) for _ in range(2)]
    rst_res = [Res("B_rst0"), Res("B_rst1")]
    gn_res = Res("B_gn")
    B = cx.banks
    BR = cx.bank_res
    T.op("pool", ("memset", dict(ap=state_f[0:32, :], constant=0.0)), writes=[st_res])
    T.op("pool", ("memset", dict(ap=state_bf[0:32, :], constant=0.0)), writes=[stb_res])

    def ret_gen():
        for i in range(64):
            ch = i // 4
            cs = slice(i * 128, (i + 1) * 128)
            im, imr = innerM[i % 2], inner_res[i % 2]
            kzb, kzr = kz[i % 2], kz_res[i % 2]
            _mm(T, B[4][:, 0:128], RK[:, cs], RQ[:, cs], True, True, [rk_res[ch], rq_res[ch]], [BR[4]], inc=False)
            _mm(T, B[4][:, 128:160], RK[:, cs], i32, True, True, [rk_res[ch], cbb_res], [BR[4]], inc=True)
            yield
            T.op("dve", ("tensor_tensor", dict(out=im, in0=B[4][:, 0:128], in1=decayT, op=ALU.mult)),
                 reads=[BR[4], cb_res], writes=[imr])
            T.op("dve", ("tensor_scalar", dict(out=kzb, in0=B[4][:, 128:160], scalar1=zeta, scalar2=None, op0=ALU.mult)),
                 reads=[BR[4], cb_res], writes=[kzr])
            yield
            _mm(T, B[5][0:64, 0:128], VV[:, i, 128:192], im, True, True, [vv_res[ch], imr], [BR[5]], inc=False)
            _mm(T, B[5][0:32, 128:192], kzb, VV[:, i, 128:192], True, True, [kzr, vv_res[ch]], [BR[5]], inc=True)
            _mm(T, B[4][0:64, 160:288], state_bf[0:32, :], QXI[:, cs], True, True, [stb_res, qxi_res[ch]], [BR[4]])
            yield
            ogb, ogr = og[ch % 2], og_res[ch % 2]
            j = i % 4
            T.op("dve", ("tensor_copy", dict(out=ogb2[0:64, :], in_=B[5][0:64, 0:128])), reads=[BR[5]], writes=[ogb2_res])
            T.op("dve", ("tensor_tensor", dict(out=ogb[0:64, j * 128:(j + 1) * 128], in0=ogb2[0:64, :], in1=B[4][0:64, 160:288], op=ALU.add)),
                 reads=[BR[4], ogb2_res], writes=[ogr[j]])
            T.op("dve", ("scalar_tensor_tensor", dict(out=state_f[0:32, :], in0=state_f[0:32, :], scalar=cd, in1=B[5][0:32, 128:192],
                                                      op0=ALU.mult, op1=ALU.add)), reads=[BR[5], st_res, cb_res], writes=[st_res])
            T.op("pool", ("tensor_copy", dict(out=state_bf[0:32, :], in_=state_f[0:32, :])), reads=[st_res], writes=[stb_res])
            yield
            if i % 4 == 3:
                cols = slice(ch * W, (ch + 1) * W)
                o64 = ogb[0:64, :]
                _mm(T, B[7][0:64, :], K.ones64[0:64, :], o64, True, True, [K.res] + ogr, [BR[7]])
                T.op("dve", ("tensor_tensor", dict(out=oc[0:64, :], in0=o64, in1=B[7][0:64, :], op=ALU.subtract)), reads=ogr + [BR[7]], writes=[gn_res])
                T.op("pool", ("tensor_tensor", dict(out=sq[0:64, :], in0=oc[0:64, :], in1=oc[0:64, :], op=ALU.mult)), reads=[gn_res], writes=[gn_res])
                yield
                _mm(T, B[7][0:64, :], K.ones64[0:64, :], sq[0:64, :], True, True, [K.res, gn_res], [BR[7]])
                T.op("act", ("activation", dict(out=rs[0:64, :], in_=B[7][0:64, :], func=AF.Ln, bias=K.eps[0:64, :], scale=1.0)), reads=[BR[7], K.res], writes=[gn_res])
                T.op("act", ("activation", dict(out=rs[0:64, :], in_=rs[0:64, :], func=AF.Exp, scale=-0.5)), reads=[gn_res], writes=[gn_res])
                T.op("dve", ("tensor_copy", dict(out=g0[0:64, :], in_=gT[:, cols])), reads=[g_res[ch]], writes=[gn_res])
                T.op("act", ("activation", dict(out=e1[0:64, :], in_=g0[0:64, :], func=AF.Exp, scale=-1.0)), reads=[gn_res], writes=[gn_res])
                yield
                T.op("dve", ("tensor_scalar", dict(out=e1[0:64, :], in0=e1[0:64, :], scalar1=1.0, scalar2=None, op0=ALU.add)), reads=[gn_res], writes=[gn_res])
                T.op("dve", ("reciprocal", dict(out=e1[0:64, :], in_=e1[0:64, :])), reads=[gn_res], writes=[gn_res])
                T.op("pool", ("tensor_tensor", dict(out=g0[0:64, :], in0=g0[0:64, :], in1=e1[0:64, :], op=ALU.mult)), reads=[gn_res], writes=[gn_res])
                T.op("dve", ("scalar_tensor_tensor", dict(out=t1[0:64, :], in0=oc[0:64, :], scalar=cb[0:64, CB_RETG:CB_RETG + 1], in1=rs[0:64, :],
                                                          op0=ALU.mult, op1=ALU.mult)), reads=[gn_res, cb_res], writes=[gn_res])
                rsb, rsr = rstage[ch % 2], rst_res[ch % 2]
                T.op("pool", ("tensor_tensor", dict(out=rsb[0:64, :], in0=t1[0:64, :], in1=g0[0:64, :], op=ALU.mult)), reads=[gn_res], writes=[rsr])
                T.dma("sp", ("dma_start", dict(out=ocat_out[128:192, OPAD + ch * W:OPAD + (ch + 1) * W], in_=rsb[0:64, :])),
                      reads=[rsr], writes=[ocat_res])
                yield

    rg = ret_gen()
    steps = [(qi, kt) for qi in range(S // QC) for kt in range(2 * qi + 2)]

    def qk(s):
        qi, kt = steps[s]
        sb, sr = B[s % 2], BR[s % 2]
        qs = slice(qi * QC, (qi + 1) * QC)
        ks = slice(kt * 128, (kt + 1) * 128)
        rd = [q_res[qi // 2], q1_res[qi // 2], k_res[kt // 4], qz_res]
        _mm(T, sb[:, 0:QC], kT[:, ks], qT0[:, qs], True, True, rd, [sr], inc=False)
        _mm(T, sb[:, QC:2 * QC], kT[:, ks], qT1[:, qs], True, True, rd, [sr], inc=True)

    qk(0)
    for s, (qi, kt) in enumerate(steps):
        nk = 2 * qi + 2
        if s + 1 < len(steps):
            qk(s + 1)
        sb, sr = B[s % 2], BR[s % 2]
        P, Pr = Pb[s % 3], P_res[s % 3]
        T.op("act", ("activation", dict(out=P, in_=sb[:, :], func=AF.Exp, scale=0.125)), reads=[sr], writes=[Pr])
        if kt >= 2 * qi:
            mk = masks[kt - 2 * qi]
            T.op("dve", ("tensor_tensor", dict(out=P, in0=P, in1=mk, op=ALU.mult)), reads=[Pr, cbb_res], writes=[Pr])
        _mm(T, B[2][:, :], VV[:, kt, 0:128], P, kt == 0, kt == nk - 1, [vv_res[kt // 4], Pr], [BR[2]], inc=False)
        _mm(T, B[3][:, :], K.ones_bf, P, kt == 0, kt == nk - 1, [K.res, Pr], [BR[3]], inc=True)
        if s % 3 == 0:
            next(rg, None)
        if kt == nk - 1:
            T.op("dve", ("reciprocal", dict(out=Rb, in_=B[3][:, :])), reads=[BR[3]], writes=[ep_res])
            T.op("dve", ("tensor_tensor", dict(out=tb, in0=B[2][:, :], in1=Rb, op=ALU.mult)), reads=[BR[2], ep_res], writes=[ep_res])
            T.op("dve", ("scalar_tensor_tensor", dict(out=ob, in0=tb[:, QC:2 * QC], scalar=neg_lam, in1=tb[:, 0:QC],
                                                      op0=ALU.mult, op1=ALU.add)), reads=[ep_res, sc_res], writes=[ep_res])
            T.op("pool", ("tensor_tensor", dict(out=sqo, in0=ob, in1=ob, op=ALU.mult)), reads=[ep_res], writes=[ep_res])
            _mm(T, B[6][:, 0:QC], K.onesF, sqo, True, True, [K.res, ep_res], [BR[6]])
            T.op("act", ("activation", dict(out=l1, in_=B[6][:, 0:QC], func=AF.Ln, bias=K.eps, scale=1.0)), reads=[BR[6], K.res], writes=[ep_res])
            T.op("act", ("activation", dict(out=l1, in_=l1, func=AF.Exp, scale=-0.5)), reads=[ep_res], writes=[ep_res])
            st, h = ostage[(qi // 2) % 2], qi % 2
            T.op("dve", ("scalar_tensor_tensor", dict(out=st[:, h * QC:(h + 1) * QC], in0=ob, scalar=gda, in1=l1,
                                                      op0=ALU.mult, op1=ALU.mult)),
                 reads=[ep_res, sc_res], writes=[ost_res[(qi // 2) % 2][h]])
            if h == 1:
                g = qi // 2
                T.dma("sp", ("dma_start", dict(out=ocat_out[0:128, OPAD + g * W:OPAD + (g + 1) * W], in_=st)),
                      reads=ost_res[(qi // 2) % 2], writes=[ocat_res])
    for _ in rg:
        pass
    cx.release(m2)
    cx.release(m)


def _finish(cx):
    T = cx.T
    toks = [('c', n, e.count) for n, e in T.eng.items() if e.count > 0] + list(T.dma_last.values())
    T.wait_all("sp", toks)


def _load_x(cx, xin):
    T = cx.T
    xT = cx.alloc(8 * TCOL).rearrange("p (c n) -> p c n", c=8)
    nblk = NCH_C // BLK_C
    bw = CW * BLK_C
    xres = [Res(f"x{b}") for b in range(nblk)]
    xv = xin.rearrange("(c p) n -> p c n", p=128)
    for b in range(nblk):
        T.dma("sp", ("dma_start", dict(out=xT[:, :, b * bw:(b + 1) * bw], in_=xv[:, :, b * bw:(b + 1) * bw])), writes=[xres[b]])
    return xT, xres


def build_prog_A():
    nc = bass.Bass("TRN2", target_bir_lowering=False)
    with ExitStack() as st:
        cx = Ctx(nc, st)
        xin = cx.din("xT", [D, TCOL])
        g = cx.din("g_pre", [128, 8])
        hout = cx.dout("hT", [D, TOK], BF16)
        K = Consts(cx)
        xT, xres = _load_x(cx, xin)
        gains, g_res = load_gains(cx, K, g, 8)
        emit_phase_A(cx, K, xT, xres, gains, g_res, hout, Res("hT_out"))
        _finish(cx)
        cx.T.build(nc, st)
    return nc


def build_prog_B(parts="123e"):
    nc = bass.Bass("TRN2", target_bir_lowering=False)
    with ExitStack() as st:
        cx = Ctx(nc, st)
        hT_full = cx.din("hT_full", [NR, D, TOK], BF16)
        wh = cx.din("wh", [D, 1024])
        rot = cx.din("rot_tab", [16, 128, 2048])
        cb = cx.din("cb", [128, CB_N])
        cbb = cx.din("cbb", [128, CBB_N])
        oc = cx.dout("ocat", [256, OPAD + S], BF16)
        K = Consts(cx)
        emit_phase_B(cx, K, hT_full, Res("h_in"), wh, rot, cb, cbb, oc, Res("oc_out"), parts=parts)
        _finish(cx)
        cx.T.build(nc, st)
    return nc


def build_prog_CA():
    nc = bass.Bass("TRN2", target_bir_lowering=False)
    with ExitStack() as st:
        cx = Ctx(nc, st)
        xin = cx.din("xT", [D, TCOL])
        oc = cx.din("oc", [D, TCOL], BF16)
        wout = cx.din("wout", [8, 128, 8, 128])
        wup = cx.din("wup", [NPAIR, 128, 8, 256])
        wdown = cx.din("wdown", [8, 128, NPAIR, 128])
        cw = cx.din("cw", [128, NPAIR * 8])
        gains = cx.din("gains", [128, 24])
        mask = cx.din("mask", [128, 8])
        gnext = cx.din("g_pre", [128, 8])
        xout = cx.dout("xT_out", [D, TCOL])
        hout = cx.dout("hT", [D, TOK], BF16)
        K = Consts(cx)
        xT, xres = _load_x(cx, xin)
        emit_phase_C(cx, K, xT, xres, oc.rearrange("(k p) n -> p k n", p=128), Res("oc_in"), wout, wup, wdown, cw, gains, mask)
        T = cx.T
        xo = xout.rearrange("(c p) n -> p c n", p=128)
        bw = CW * BLK_C
        xo_res = Res("x_out")
        for b in range(NCH_C // BLK_C):
            T.dma("sp", ("dma_start", dict(out=xo[:, :, b * bw:(b + 1) * bw], in_=xT[:, :, b * bw:(b + 1) * bw])),
                  reads=[xres[b]], writes=[xo_res])
        g2, g2_res = load_gains(cx, K, gnext, 8)
        emit_phase_A(cx, K, xT, xres, g2, g2_res, hout, Res("hT_out"))
        _finish(cx)
        cx.T.build(nc, st)
    return nc


def _pc(v):
    return np.ascontiguousarray(np.asarray(v, np.float32).reshape(-1, 128).T)


def _rot_tables():
    pos = np.arange(S, dtype=np.float32)
    inv_da = (np.float32(500000.0) ** (-np.arange(0, 16, 2, dtype=np.float32) / np.float32(16))).astype(np.float32)
    inv_r = (np.float32(10000.0) ** (-np.arange(0, 32, 2, dtype=np.float32) / np.float32(32))).astype(np.float32)
    a_da = (pos[:, None] * inv_da[None, :]).astype(np.float32)
    a_r = (pos[:, None] * inv_r[None, :]).astype(np.float32)
    cda, sda = np.cos(a_da).astype(np.float32), np.sin(a_da).astype(np.float32)
    cr, sr = np.cos(a_r).astype(np.float32), np.sin(a_r).astype(np.float32)
    tab = np.zeros((128, 4, S), np.float32)
    tab[:, 0, :] = 1.0
    for base in (0, 64):
        for d in range(16):
            tab[base + d, 0, :] = cda[:, d % 8]
            tab[base + d, 1, :] = (-sda[:, d] if d < 8 else sda[:, d - 8])
    sc = np.float32(32.0 ** -0.5)
    for d in range(32):
        c = cr[:, d % 16]
        s_ = (-sr[:, d] if d < 16 else sr[:, d - 16])
        tab[d, 2, :] = c
        tab[d, 3, :] = s_
        tab[32 + d, 2, :] = c * sc
        tab[32 + d, 3, :] = s_ * sc
    t = tab.reshape(128, 4, 16, 512).transpose(2, 0, 1, 3).reshape(16, 128, 2048)
    return np.ascontiguousarray(t)


def _head_cols(h):
    def q_da(m, d):
        return 0 + h * 128 + m * 64 + d

    def k_da(m, d):
        return 512 + h * 128 + m * 64 + d

    def partner_da(d):
        return d + 8 if d < 8 else (d - 8 if d < 16 else d)

    def partner_r(d):
        return d + 16 if d < 16 else d - 16
    cols = []
    for f in (q_da, k_da):
        cols += [f(m, d) for m in range(2) for d in range(64)]
        cols += [f(m, partner_da(d)) for m in range(2) for d in range(64)]
    qr = [1536 + h * 32 + d for d in range(32)]
    kr = [1664 + h * 32 + d for d in range(32)]
    gr = [2048 + h * 64 + d for d in range(64)]
    cols += qr + kr + gr
    cols += [1536 + h * 32 + partner_r(d) for d in range(32)] + [1664 + h * 32 + partner_r(d) for d in range(32)] + gr
    cols += [1024 + h * 128 + d for d in range(128)] + [1792 + h * 64 + d for d in range(64)] + [2304 + h * 64 + d for d in range(64)]
    assert len(cols) == 1024
    return np.array(cols)


def _consts_B(h, l, diff_subln, ret_norm, pool_scale, pool_w, lq1, lk1, lq2, lk2):
    cb = np.zeros((128, CB_N), np.float32)
    gam = 1.0 - 2.0 ** (-5.0 - h)
    lg = np.float32(np.log(np.float32(gam)))
    idx = np.arange(128, dtype=np.float32)
    diff = idx[None, :] - idx[:, None]
    dec = np.where(diff >= 0, np.exp(np.maximum(diff, 0) * lg), 0.0).astype(np.float32)
    cb[:, CB_DECAY:CB_DECAY + 128] = dec
    cb[:, CB_ZETA] = np.exp((127.0 - idx) * lg)
    cb[:, CB_CD] = np.exp(np.float32(128.0) * lg)
    lam_init = 0.8 - 0.6 * math.exp(-0.3 * l)
    cb[:, CB_LAMI] = lam_init
    cb[:, CB_OML] = 1.0 - lam_init
    cb[:, CB_SUBLN] = diff_subln
    cb[0:64, CB_PSCALE] = pool_scale[h * 64:(h + 1) * 64]
    cb[0:64, CB_RETG] = ret_norm[h * 64:(h + 1) * 64]
    xi = np.exp((idx + 1.0) * lg).astype(np.float32)
    cb[:, CB_XI:CB_XI + 512] = np.tile(xi, 4)[None, :]
    w = (2, 4, 8, 16)[h]
    s_ = np.arange(128)[:, None]
    t_ = np.arange(128)[None, :]
    band = ((t_ - s_ >= 0) & (t_ - s_ < w)).astype(np.float32)
    cb[:, CB_BCUR:CB_BCUR + 128] = band / w - np.eye(128, dtype=np.float32)
    cb[:, CB_BPREV:CB_BPREV + 128] = ((t_ - s_ + 128 >= 0) & (t_ - s_ + 128 < w)).astype(np.float32) / w
    cb[:, CB_BCUR0:CB_BCUR0 + 128] = band / np.minimum(t_ + 1.0, float(w)) - np.eye(128, dtype=np.float32)
    cb[:, CB_LAMV:CB_LAMV + 256] = np.concatenate([lq1, lk1, lq2, lk2])[None, :]
    cbb = np.zeros((128, CBB_N), np.float32)
    cbb[0:32, CBB_I32:CBB_I32 + 32] = np.eye(32, dtype=np.float32)
    j = np.arange(256)[None, :]
    k = np.arange(128)[:, None]
    m0 = (j >= k).astype(np.float32)
    m1 = (j >= k + 128).astype(np.float32)
    cbb[:, CBB_MASK:CBB_MASK + 512] = np.concatenate([m0, m0], axis=1)
    cbb[:, CBB_MASK + 512:CBB_MASK + 1024] = np.concatenate([m1, m1], axis=1)
    cbb[0:64, CBB_PW:CBB_PW + 64] = pool_w[h]
    return cb, cbb


def _wout_rows():
    rows = []
    for r in range(NR):
        rows += [r * 128 + d for d in range(128)] + [512 + r * 64 + d for d in range(64)] + [768 + r * 64 + d for d in range(64)]
    return np.array(rows)


def _prep_layer(inp, l):
    P = {}
    w_in = np.asarray(inp["w_in"][l], np.float32)
    P["wh"] = [np.ascontiguousarray(w_in[:, _head_cols(h)]) for h in range(NR)]
    P["cB"] = [_consts_B(h, l, inp["diff_subln"][l], inp["ret_norm"][l], inp["pool_scale"][l], inp["pool_w"][l],
                         inp["lambda_q1"][l], inp["lambda_k1"][l], inp["lambda_q2"][l], inp["lambda_k2"][l]) for h in range(NR)]
    wo = np.asarray(inp["w_out"][l], np.float32)[_wout_rows(), :]
    P["wout"] = np.ascontiguousarray(wo.reshape(8, 128, 8, 128).transpose(2, 1, 0, 3))
    wu = np.asarray(inp["w_up"][l], np.float32)
    g = wu[:, :DFF].reshape(8, 128, NPAIR, 128)
    v = wu[:, DFF:].reshape(8, 128, NPAIR, 128)
    P["wup"] = np.ascontiguousarray(np.concatenate([g, v], axis=3).transpose(2, 1, 0, 3))
    wd = np.asarray(inp["w_down"][l], np.float32)
    P["wdown"] = np.ascontiguousarray(wd.reshape(NPAIR, 128, 8, 128).transpose(2, 1, 0, 3))
    cwl = np.asarray(inp["conv_w"][l], np.float32)
    cbl = np.asarray(inp["conv_b"][l], np.float32)
    cw = np.zeros((128, NPAIR, 2, 4), np.float32)
    for gv in range(2):
        blk = cwl[:, gv * DFF:(gv + 1) * DFF].reshape(3, NPAIR, 128)
        cw[:, :, gv, 0:3] = blk.transpose(2, 1, 0)
        cw[:, :, gv, 3] = cbl[gv * DFF:(gv + 1) * DFF].reshape(NPAIR, 128).T
    P["cw"] = np.ascontiguousarray(cw.reshape(128, NPAIR * 8))
    P["gains"] = np.ascontiguousarray(np.concatenate([_pc(inp["norm_mix_post"][l]), _pc(inp["norm_mlp_pre"][l]),
                                                      _pc(inp["norm_mlp_post"][l])], axis=1))
    P["g_pre"] = _pc(inp["norm_mix_pre"][l])
    return P


def _x_shards(x):
    out = []
    for b in range(NB):
        for r in range(NR):
            xt = np.zeros((D, TCOL), np.float32)
            lo = r * TOK - HALO
            if lo < 0:
                xt[:, HALO:] = x[b, 0:TOK, :].T
            else:
                xt[:] = x[b, lo:lo + TCOL, :].T
            out.append(xt)
    return out


_PROGS = {}


def _prog(name):
    if name not in _PROGS:
        _PROGS[name] = {"A": build_prog_A, "B": build_prog_B, "CA": build_prog_CA}[name]()
    return _PROGS[name]


def kernel(**inp):
    import ml_dtypes
    inp = {k: np.asarray(v) for k, v in inp.items()}
    x = inp["x"].astype(np.float32, copy=False)
    cores = list(range(8))
    xs = _x_shards(x)
    masks = [np.full((128, 8), 0.0 if (c % NR) == 0 else 1.0, np.float32) for c in cores]
    rot = _rot_tables()
    layers = [_prep_layer(inp, l) for l in range(DEPTH)]
    res = run_bass_kernel_spmd(_prog("A"), [{"xT": xs[c], "g_pre": layers[0]["g_pre"]} for c in cores], core_ids=cores)
    hT = [res.results[c]["hT"] for c in cores]
    for l in range(DEPTH):
        L = layers[l]
        ins = []
        for c in cores:
            b, r = divmod(c, NR)
            hfull = np.ascontiguousarray(np.stack([hT[b * NR + k] for k in range(NR)], axis=0))
            ins.append({"hT_full": hfull, "wh": L["wh"][r], "rot_tab": rot, "cb": L["cB"][r][0], "cbb": L["cB"][r][1]})
        res = run_bass_kernel_spmd(_prog("B"), ins, core_ids=cores)
        oc = [res.results[c]["ocat"] for c in cores]
        ins = []
        gnext = layers[l + 1]["g_pre"] if l + 1 < DEPTH else L["g_pre"]
        for c in cores:
            b, r = divmod(c, NR)
            ocs = np.ascontiguousarray(np.concatenate([oc[b * NR + k][:, OPAD - HALO + r * TOK:OPAD - HALO + r * TOK + TCOL] for k in range(NR)], axis=0))
            ins.append({"xT": xs[c], "oc": ocs, "wout": L["wout"], "wup": L["wup"], "wdown": L["wdown"], "cw": L["cw"],
                        "gains": L["gains"], "mask": masks[c], "g_pre": gnext})
        res = run_bass_kernel_spmd(_prog("CA"), ins, core_ids=cores)
        xs = [res.results[c]["xT_out"] for c in cores]
        hT = [res.results[c]["hT"] for c in cores]
    out = np.zeros((NB, S, D), np.float32)
    for c in cores:
        b, r = divmod(c, NR)
        out[b, r * TOK:(r + 1) * TOK, :] = xs[c][:, HALO:].T
    return out
```
